# Optimizing a Trainium2 kernel written in Bass

```python
import jax, jax.numpy as jnp
from jax import lax
import numpy as np

D_MODEL = 1024
BATCH = 8
SEQ = 2048
DEPTH = 1

MIX_WIDTH = D_MODEL
HEAD_DIM = 64
CONV_WIDTH = MIX_WIDTH // 2
CFM_WIDTH = MIX_WIDTH - CONV_WIDTH
N_HEADS = MIX_WIDTH // HEAD_DIM
SHORT_K = 3
CFM_K = 31
IN_COLS = 3 * CONV_WIDTH + 2 * CFM_WIDTH
FFN_HIDDEN = ((8 * D_MODEL // 3 + 255) // 256) * 256
N_MOD = 6
EPS = 1e-6

kernel_name = "hybrid_shortconv_conformer_adaln_block"


def rmsnorm(x, g):
    xf = x.astype(jnp.float32)
    y = xf * lax.rsqrt(jnp.mean(xf * xf, axis=-1, keepdims=True) + EPS)
    return (y * g.astype(jnp.float32)).astype(x.dtype)


def layernorm(x, g, b):
    xf = x.astype(jnp.float32)
    mu = jnp.mean(xf, axis=-1, keepdims=True)
    var = jnp.mean(jnp.square(xf - mu), axis=-1, keepdims=True)
    y = (xf - mu) * lax.rsqrt(var + EPS)
    return (y * g.astype(jnp.float32) + b.astype(jnp.float32)).astype(x.dtype)


def causal_dwconv(u, w, b):
    k = w.shape[0]
    out = lax.conv_general_dilated(
        u, w[:, None, :].astype(u.dtype), window_strides=(1,), padding=[(k - 1, 0)],
        dimension_numbers=("NWC", "WIO", "NWC"), feature_group_count=u.shape[-1])
    return out + b.astype(u.dtype)


def modulate(h, shift, scale):
    return h * (1.0 + scale[:, None, :]) + shift[:, None, :]


def setup_inputs(seed: int = 0) -> dict:
    key = jax.random.key(seed)
    ks = jax.random.split(key, 24)
    f32 = jnp.float32
    nrm = lambda k, shape, s: jax.random.normal(k, shape, f32) * s
    L = DEPTH
    return {
        "x": jax.random.normal(ks[0], (BATCH, SEQ, D_MODEL), f32),
        "c": jax.random.normal(ks[1], (BATCH, D_MODEL), f32),
        "w_ada": nrm(ks[2], (L, D_MODEL, N_MOD * D_MODEL), 0.5 * D_MODEL ** -0.5),
        "b_ada": nrm(ks[3], (L, N_MOD * D_MODEL), 0.02),
        "g_pre_mix": 1.0 + nrm(ks[4], (L, D_MODEL), 0.02),
        "g_post_mix": 1.0 + nrm(ks[5], (L, D_MODEL), 0.02),
        "w_in": nrm(ks[6], (L, D_MODEL, IN_COLS), D_MODEL ** -0.5),
        "w_short": nrm(ks[7], (L, SHORT_K, CONV_WIDTH), SHORT_K ** -0.5),
        "b_short": nrm(ks[8], (L, CONV_WIDTH), 0.02),
        "w_cfm_dw": nrm(ks[9], (L, CFM_K, CFM_WIDTH), CFM_K ** -0.5),
        "b_cfm_dw": nrm(ks[10], (L, CFM_WIDTH), 0.02),
        "g_cfm_ln": 1.0 + nrm(ks[11], (L, CFM_WIDTH), 0.02),
        "b_cfm_ln": nrm(ks[12], (L, CFM_WIDTH), 0.02),
        "beta_mix": 1.0 + nrm(ks[13], (L, MIX_WIDTH), 0.02),
        "w_out": nrm(ks[14], (L, MIX_WIDTH, D_MODEL), MIX_WIDTH ** -0.5),
        "g_pre_ffn": 1.0 + nrm(ks[15], (L, D_MODEL), 0.02),
        "g_post_ffn": 1.0 + nrm(ks[16], (L, D_MODEL), 0.02),
        "w_gate_up": nrm(ks[17], (L, D_MODEL, 2 * FFN_HIDDEN), D_MODEL ** -0.5),
        "w_down": nrm(ks[18], (L, FFN_HIDDEN, D_MODEL), FFN_HIDDEN ** -0.5),
    }


def token_mixer(h, w_in, w_short, b_short, w_cfm_dw, b_cfm_dw, g_cfm_ln, b_cfm_ln,
                beta_mix, w_out):
    z = jnp.einsum("bsd,dk->bsk", h, w_in)
    gb, gc, v, a, g = jnp.split(
        z, [CONV_WIDTH, 2 * CONV_WIDTH, 3 * CONV_WIDTH, 3 * CONV_WIDTH + CFM_WIDTH], axis=-1)
    y_short = gb * causal_dwconv(gc * v, w_short, b_short)
    u = a * jax.nn.sigmoid(g)
    u = causal_dwconv(u, w_cfm_dw, b_cfm_dw)
    u = layernorm(u, g_cfm_ln, b_cfm_ln)
    y_cfm = jax.nn.silu(u)
    y = jnp.concatenate([y_short, y_cfm], axis=-1)
    bsz, s = y.shape[0], y.shape[1]
    yh = y.reshape(bsz, s, N_HEADS, HEAD_DIM).astype(jnp.float32)
    yh = yh * lax.rsqrt(jnp.mean(yh * yh, axis=-1, keepdims=True) + EPS)
    y = (yh.reshape(bsz, s, MIX_WIDTH) * beta_mix.astype(jnp.float32)).astype(h.dtype)
    return jnp.einsum("bsm,md->bsd", y, w_out)


def swiglu_ffn(h, w_gate_up, w_down):
    gu = jnp.einsum("bsd,df->bsf", h, w_gate_up)
    gate, up = jnp.split(gu, 2, axis=-1)
    return jnp.einsum("bsf,fd->bsd", jax.nn.silu(gate) * up, w_down)


def reference(x, c, w_ada, b_ada, g_pre_mix, g_post_mix, w_in, w_short, b_short,
              w_cfm_dw, b_cfm_dw, g_cfm_ln, b_cfm_ln, beta_mix, w_out,
              g_pre_ffn, g_post_ffn, w_gate_up, w_down):
    c_act = jax.nn.silu(c)
    for l in range(DEPTH):
        mod = jnp.einsum("bd,dk->bk", c_act, w_ada[l]) + b_ada[l]
        sh1, sc1, gt1, sh2, sc2, gt2 = jnp.split(mod, N_MOD, axis=-1)
        h = modulate(rmsnorm(x, g_pre_mix[l]), sh1, sc1)
        o = token_mixer(h, w_in[l], w_short[l], b_short[l], w_cfm_dw[l], b_cfm_dw[l],
                        g_cfm_ln[l], b_cfm_ln[l], beta_mix[l], w_out[l])
        x = x + gt1[:, None, :] * rmsnorm(o, g_post_mix[l])
        h = modulate(rmsnorm(x, g_pre_ffn[l]), sh2, sc2)
        f = swiglu_ffn(h, w_gate_up[l], w_down[l])
        x = x + gt2[:, None, :] * rmsnorm(f, g_post_ffn[l])
    return x
```

```python
import numpy as np
from contextlib import ExitStack
import concourse.bass as bass
import concourse.mybir as mybir
from concourse.bass_utils import run_bass_kernel_spmd

F32 = mybir.dt.float32
BF16 = mybir.dt.bfloat16
ALU = mybir.AluOpType
AF = mybir.ActivationFunctionType

D = 1024
S = 2048
NCORE = 8
NS = 2
TB = NS * 512
NB = S // TB
NT = TB // 128
HID = 2816
NJ = HID // 128
EPS = 1e-6
RING = 4
SLOT_F = 5120

V_C = 0
V_BADA = 8
V_GPRE1 = 56
V_GPOST1 = 64
V_GPRE2 = 72
V_GPOST2 = 80
V_BETA = 88
V_WSH = 96
V_BSH = 108
V_WCF = 112
V_BCF = 236
V_GLN = 240
V_BLN = 244
V_ROWS = 256

DEBUG = {}
STOP = {"at": None}


class _StopBuild(Exception):
    pass


class Reg:
    __slots__ = ("name", "w", "rs", "fence", "psum")

    def __init__(self, name, psum=False):
        self.name = name
        self.w = None
        self.rs = []
        self.fence = ()
        self.psum = psum


class DmaSem:
    __slots__ = ("sem", "count", "name")

    def __init__(self, sem, name):
        self.sem = sem
        self.count = 0
        self.name = name


class Op:
    __slots__ = ("eng", "calls", "deps", "sig", "val", "dsem", "dval", "name", "idx")


class _Rec:
    def __init__(self):
        self.calls = []

    def __getattr__(self, name):
        def f(*a, **kw):
            self.calls.append((name, a, kw))
            return self
        return f


class Prog:
    ENGS = ("pe", "act", "dve", "pool", "sp")

    def __init__(self, nc):
        self.nc = nc
        self.q = {e: [] for e in self.ENGS}
        self.n = 0
        self.dsems = []

    def reg(self, name, psum=False):
        return Reg(name, psum)

    def regs_n(self, name, n):
        return [Reg("%s%d" % (name, i)) for i in range(n)]

    def dsem(self, sem, name):
        d = DmaSem(sem, name)
        self.dsems.append(d)
        return d

    def fence(self, regs_old, regs_new):
        ops = []
        for r in regs_old:
            if r.w is not None:
                ops.append(r.w)
            ops.extend(r.rs)
            ops.extend(r.fence)
        last = {}
        keep = []
        for o in ops:
            if o.dsem is not None:
                keep.append(o)
            elif o.eng not in last or o.idx > last[o.eng].idx:
                last[o.eng] = o
        keep.extend(last.values())
        for r in regs_new:
            r.fence = tuple(keep)

    def op(self, eng, fn, reads=(), writes=(), dsem=None, name=""):
        o = Op()
        o.eng = eng
        rec = _Rec()
        fn(rec)
        o.calls = rec.calls
        o.sig = False
        o.val = None
        o.dsem = dsem
        o.name = name
        o.idx = self.n
        self.n += 1
        deps = {}
        for r in reads:
            if r.w is not None:
                deps[id(r.w)] = r.w
            for f in r.fence:
                deps[id(f)] = f
            if r.psum:
                for x in r.rs:
                    if x.eng != eng:
                        deps[id(x)] = x
        for r in writes:
            if r.w is not None:
                deps[id(r.w)] = r.w
            for x in r.rs:
                deps[id(x)] = x
            for f in r.fence:
                deps[id(f)] = f
        dl = []
        for d in deps.values():
            if d.dsem is not None:
                dl.append((d, d.dsem.count))
            else:
                if d.eng == "pe" and eng == "pe" and dsem is None:
                    continue
                d.sig = True
                dl.append((d, None))
        o.deps = dl
        if dsem is not None:
            dsem.count += 16
            o.dval = dsem.count
        else:
            o.dval = None
        for r in reads:
            r.rs.append(o)
        for r in writes:
            r.w = o
            r.rs = []
        self.q[eng].append(o)
        return o

    def emit(self, esems, final_eng="sp"):
        nc = self.nc
        for e in self.ENGS:
            c = 0
            for o in self.q[e]:
                if o.dsem is None and o.sig:
                    c += 1
                    o.val = c
        prog = self

        def run(e, eng):
            waited = {}
            for o in prog.q[e]:
                for d, dv in o.deps:
                    if d.dsem is not None:
                        sem, val = d.dsem.sem, dv
                    else:
                        sem, val = esems[d.eng], d.val
                    k = id(sem)
                    if waited.get(k, 0) < val:
                        eng.wait_ge(sem, val)
                        waited[k] = val
                ins = None
                for (mname, a, kw) in o.calls:
                    ins = getattr(eng, mname)(*a, **kw)
                if o.dsem is not None:
                    ins.then_inc(o.dsem.sem, 16)
                elif o.sig:
                    ins.then_inc(esems[e], 1)
            if e == final_eng:
                for d in prog.dsems:
                    if d.count > 0:
                        eng.wait_ge(d.sem, d.count)

        with nc.Block() as block:

            @block.tensor
            def _(eng):
                run("pe", eng)

            @block.scalar
            def _(eng):
                run("act", eng)

            @block.vector
            def _(eng):
                run("dve", eng)

            @block.gpsimd
            def _(eng):
                run("pool", eng)

            @block.sync
            def _(eng):
                run("sp", eng)


class Pool:
    def __init__(self, bufs):
        self.bufs = bufs
        self.i = 0

    def next(self):
        b = self.bufs[self.i % len(self.bufs)]
        self.i += 1
        return b


def build_program(debug=False):
    nc = bass.Bass("TRN2", target_bir_lowering=False)
    x_d = nc.dram_tensor("x", [S, D], F32, kind="ExternalInput").ap()
    v_d = nc.dram_tensor("vmat", [V_ROWS, 128], F32, kind="ExternalInput").ap()
    wada_d = nc.dram_tensor("wada", [12, 128, 4096], F32, kind="ExternalInput").ap()
    winc_d = nc.dram_tensor("winc", [4, 128, 2048], F32, kind="ExternalInput").ap()
    wins_d = nc.dram_tensor("wins", [4, 128, 3072], F32, kind="ExternalInput").ap()
    wout_d = nc.dram_tensor("wout", [2, 128, 4096], F32, kind="ExternalInput").ap()
    wgu_d = nc.dram_tensor("wgu", [11, 128, 4096], F32, kind="ExternalInput").ap()
    wdn_d = nc.dram_tensor("wdn", [8, 128, 2816], F32, kind="ExternalInput").ap()
    out_d = nc.dram_tensor("out", [S, D], F32, kind="ExternalOutput").ap()
    dbg_d = {}

    P = Prog(nc)
    es = ExitStack()
    with es:
        def sb(name, shape, dt):
            return es.enter_context(nc.sbuf_tensor(name, shape, dt))

        def ps(name):
            return es.enter_context(nc.psum_tensor(name, [128, 512], F32))

        def sem(name):
            return es.enter_context(nc.semaphore(name))

        esems = {e: sem("s_" + e) for e in ("pe", "act", "dve", "pool")}

        ident_f = sb("ident_f", [128, 128], F32)
        ident_b = sb("ident_b", [128, 128], BF16)
        ones_ln = sb("ones_ln", [128, 128], BF16)
        ones_d = sb("ones_d", [128, 128], BF16)
        blk = sb("blk", [128, 128], BF16)
        eps_t = sb("eps_t", [128, 1], F32)
        one_t = sb("one_t", [128, 1], F32)
        v_sb = sb("v_sb", [128, 2, 128], F32)
        vt = sb("vt", [128, V_ROWS], F32)
        nvt = sb("nvt", [128, 8], F32)
        ctmp = sb("ctmp", [128, 8], F32)
        modT = sb("modT", [128, 48], F32)
        gs = sb("gs", [128, 16], F32)
        gtg = sb("gtg", [128, 16], F32)
        cact_m = sb("cact_m", [128, 8, 128], BF16)
        onesb = sb("onesb", [128, 128], BF16)
        adscr = sb("adscr", [128, 128], F32)
        adsum = sb("adsum", [128, 4], F32)
        dg = sb("dg", [128, 2, 31, 128], BF16)
        dg3 = sb("dg3", [128, 4, 3, 128], BF16)
        ssq = sb("ssq", [128, NT], F32)
        rstd = sb("rstd", [128, NT], F32)

        r_const = P.reg("const")
        r_vsb = P.reg("v_sb")
        r_vt = P.reg("vt")
        r_cact = P.reg("cact")
        r_ctmp = P.reg("ctmp")
        r_mod = [P.reg("mod%d" % i) for i in range(6)]
        r_gs = [P.reg("gs0"), P.reg("gs1")]
        r_gtg = [P.reg("gtg0"), P.reg("gtg1")]
        r_adscr = P.reg("adscr")
        r_dg = [P.reg("dg0"), P.reg("dg1")]
        r_dg3 = P.reg("dg3")
        r_ssq = P.regs_n("ssq", NS)
        r_rstd = P.regs_n("rstd", NS)

        xb = sb("xb", [128, NT, D], F32)
        assert NS == 2
        hy = sb("hy", [128, 16, TB], BF16)
        hT = hy[:, 0:8, :]
        yT = hy[:, 8:16, :]
        hyf = hy[:].rearrange("p a t -> p (a t)").bitcast(F32).rearrange("p (s m t) -> p s m t", s=2, m=8)
        fT_F = lambda m, s: hyf[:, s, m, :]
        R = sb("R", [128, 11 * TB], F32)
        actT = R[:].bitcast(BF16).rearrange("p (j t) -> p j t", j=NJ)
        u1 = R[:, 0:4 * TB].rearrange("p (j t) -> p j t", j=4)
        o1 = 4 * TB
        cvb = R[:, o1:o1 + 4 + 2 * TB].bitcast(BF16).rearrange("p (j t) -> p j t", j=4)
        o2 = o1 + 4 + 2 * TB
        u0b = R[:, o2:o2 + 60 + 2 * TB].bitcast(BF16).rearrange("p (j t) -> p j t", j=4)
        o3 = o2 + 60 + 2 * TB
        xn_views = [R[:, o3 + 512 * i:o3 + 512 * (i + 1)].bitcast(BF16) for i in range(4)]
        o4 = o3 + 2048
        junk = R[:, o4:o4 + 512].bitcast(BF16)
        assert o4 + 512 <= 11 * TB
        fT_Cv = R[:, 0:8 * TB].rearrange("p (s m t) -> p s m t", s=2, m=8)
        fT_C = lambda m, s: fT_Cv[:, s, m, :]
        cvh_s = sb("cvh_s", [128, 4, 2], BF16)
        u0h_s = sb("u0h_s", [128, 4, 30], BF16)
        slots = [sb("slot%d" % i, [128, SLOT_F], BF16) for i in range(RING)]

        r_xb = P.regs_n("xb", NT)
        r_hT = [[P.reg("hT%d_%d" % (k, s)) for s in range(NS)] for k in range(8)]
        r_yT = [[P.reg("yT%d_%d" % (k, s)) for s in range(NS)] for k in range(8)]
        r_u1 = [[P.reg("u1%d_%d" % (j, s)) for s in range(NS)] for j in range(4)]
        r_cvb = [[P.reg("cvb%d_%d" % (j, s)) for s in range(NS)] for j in range(4)]
        r_cvh = P.regs_n("cvh", 4)
        r_cvhA = P.regs_n("cvhA", 4)
        r_u0hA = P.regs_n("u0hA", 4)
        r_u0b = [[P.reg("u0b%d_%d" % (j, s)) for s in range(NS)] for j in range(4)]
        r_u0h = P.regs_n("u0h", 4)
        r_act = [[P.reg("act%d_%d" % (j, s)) for s in range(NS)] for j in range(NJ)]
        r_fTC = [[P.reg("fTC%d_%d" % (m, s)) for s in range(NS)] for m in range(8)]
        r_fTF = [[P.reg("fTF%d_%d" % (m, s)) for s in range(NS)] for m in range(8)]
        r_slot = P.regs_n("slot", RING)

        def mkpool(name, n, shape, dt):
            return Pool([(sb("%s%d" % (name, i), shape, dt), P.reg("%s%d" % (name, i))) for i in range(n)])

        tf = mkpool("tf", 10, [128, 512], F32)
        tff = Pool(tf.bufs[2:10])
        tgb = mkpool("tgb", 2, [128, 512], F32)
        tb = mkpool("tb", 4, [128, 512], BF16)
        tb2 = mkpool("tb2", 4, [128, 512], BF16)
        txp = mkpool("tx", 1, [128, 512], F32)
        txn = Pool([(xn_views[i], P.reg("txn%d" % i)) for i in range(4)])
        r_junk = P.reg("junk")

        mmr = Pool([(ps("mm%d" % i), P.reg("mm%d" % i, True)) for i in range(4)])
        tpr = Pool([(ps("tp%d" % i), P.reg("tp%d" % i, True)) for i in range(2)])
        stb = [(ps("st%d" % i), P.reg("st%d" % i, True)) for i in range(2)]
        strr = Pool(stb)

        ds_x = [P.dsem(sem("dx%d" % i), "dx%d" % i) for i in range(NT)]
        ds_o = [P.dsem(sem("do%d" % i), "do%d" % i) for i in range(NT)]
        ds_slot = [P.dsem(sem("dw%d" % i), "dw%d" % i) for i in range(RING)]
        ds_v = P.dsem(sem("dv"), "dv")
        ds_dbg = P.dsem(sem("ddbg"), "ddbg")

        def flat(ll):
            return [r for l in ll for r in l]

        r_txn = [b[1] for b in txn.bufs]

        def vcol(row, n=1):
            return vt[:, row:row + n]

        def dump(name, ap, shape, dt, regs):
            if not debug:
                return
            t = nc.dram_tensor("dbg_" + name, list(shape), dt, kind="ExternalOutput").ap()
            dbg_d[name] = t
            P.op("sp", lambda e: e.dma_start(out=t, in_=ap), reads=regs, dsem=ds_dbg)

        sched = []
        for p in range(4):
            sched.append((wada_d[p], 4096))
        for T in range(NB):
            for j in range(4):
                sched.append((winc_d[j], 2048))
                if T == 0:
                    sched.append((wada_d[4 + j], 4096))
            for j in range(4):
                sched.append((wins_d[j], 3072))
                if T == 0:
                    sched.append((wada_d[8 + j], 4096))
            for h in range(2):
                sched.append((wout_d[h], 4096))
            for g in range(11):
                sched.append((wgu_d[g], 4096))
            for rpt in range(2 if T == NB - 1 else 1):
                for m in range(8):
                    sched.append((wdn_d[m], 2816))
        state = {"issued": 0, "next": 0, "released": 0, "hold": False}

        def prefetch():
            while state["issued"] < len(sched) and state["issued"] - RING < state["released"]:
                i = state["issued"]
                k = i % RING
                src, F = sched[i]
                P.op("pool", lambda e, k=k, src=src, F=F: e.dma_start(out=slots[k][:, 0:F], in_=src),
                     writes=[r_slot[k]], dsem=ds_slot[k], name="wload%d" % i)
                state["issued"] += 1

        def take(hold=False):
            i = state["next"]
            state["next"] += 1
            if hold:
                state["hold"] = True
            elif not state["hold"]:
                state["released"] = i
            prefetch()
            assert state["issued"] > i, (state, i)
            k = i % RING
            return slots[k], r_slot[k]

        def end_hold():
            state["hold"] = False
            state["released"] = state["next"]
            prefetch()

        P.op("sp", lambda e: e.dma_start(out=v_sb[:], in_=v_d.rearrange("(t r) p -> r t p", t=2)),
             writes=[r_vsb], dsem=ds_v)
        prefetch()

        r_idf = P.reg("ident_f")
        r_blk = P.reg("blk")
        P.op("pool", lambda e: e.memset(ident_f[:], 0.0), writes=[r_idf])
        P.op("pool", lambda e: e.affine_select(out=ident_f[:], in_=ident_f[:], compare_op=ALU.not_equal, fill=1.0,
                                               base=0, pattern=[[-1, 128]], channel_multiplier=1),
             reads=[r_idf], writes=[r_idf])
        P.op("pool", lambda e: e.memset(ones_ln[:], 1.0 / 512.0), writes=[P.reg("x1")])
        P.op("pool", lambda e: e.memset(ones_d[:], 1.0 / 1024.0), writes=[P.reg("x2")])
        P.op("pool", lambda e: e.memset(onesb[:], 1.0), writes=[P.reg("x5")])
        P.op("pool", lambda e: e.memset(blk[:], 0.0), writes=[r_blk])
        P.op("pool", lambda e: e.memset(blk[0:64, 0:64], 1.0 / 64.0), writes=[r_blk])
        P.op("pool", lambda e: e.memset(blk[64:128, 64:128], 1.0 / 64.0), writes=[r_blk])
        P.op("pool", lambda e: e.memset(eps_t[:], EPS), writes=[P.reg("x3")])
        P.op("pool", lambda e: e.memset(one_t[:], 1.0), writes=[P.reg("x4")])
        P.op("pool", lambda e: e.memset(cvh_s[:], 0.0), writes=r_cvh)
        P.op("pool", lambda e: e.memset(u0h_s[:], 0.0), writes=r_u0h)
        P.op("pool", lambda e: e.memset(ctmp[:], 0.0), reads=[r_idf, r_blk], writes=[r_const, r_ctmp])
        P.op("dve", lambda e: e.tensor_copy(out=ident_b[:], in_=ident_f[:]), reads=[r_const], writes=[r_const])

        st_t, st_r = strr.next()
        for t in range(2):
            P.op("pe", lambda e, t=t: e.transpose(out=st_t[:, t * 128:(t + 1) * 128], in_=v_sb[:, t, :], identity=ident_f[:]),
                 reads=[r_vsb, r_const], writes=[st_r])
        P.op("dve", lambda e: e.tensor_copy(out=vt[:], in_=st_t[:, 0:V_ROWS]), reads=[st_r], writes=[r_vt])
        P.op("dve", lambda e: e.tensor_scalar(out=nvt[:], in0=vt[:, V_GLN:V_GLN + 8], scalar1=-1.0, scalar2=None, op0=ALU.mult),
             reads=[r_vt], writes=[r_vt])
        P.op("act", lambda e: e.activation(out=ctmp[:], in_=vt[:, V_C:V_C + 8], func=AF.Exp, scale=-1.0), reads=[r_vt], writes=[r_ctmp])
        P.op("act", lambda e: e.activation(out=ctmp[:], in_=ctmp[:], func=AF.Ln, bias=one_t[:], scale=1.0), reads=[r_ctmp, r_const], writes=[r_ctmp])
        P.op("act", lambda e: e.activation(out=ctmp[:], in_=ctmp[:], func=AF.Exp, scale=-1.0), reads=[r_ctmp], writes=[r_ctmp])
        P.op("dve", lambda e: e.tensor_tensor(out=ctmp[:], in0=ctmp[:], in1=vt[:, V_C:V_C + 8], op=ALU.mult),
             reads=[r_ctmp, r_vt], writes=[r_ctmp])
        for kc in range(8):
            P.op("dve", lambda e, kc=kc: e.tensor_scalar(out=cact_m[:, kc, :], in0=onesb[:], scalar1=ctmp[:, kc:kc + 1],
                                                        scalar2=None, op0=ALU.mult),
                 reads=[r_ctmp, r_const], writes=[r_cact])
        for k in range(31):
            P.op("dve", lambda e, k=k: e.tensor_scalar(
                out=dg[:, 0, k, :], in0=ident_b[:], scalar1=vcol(V_WCF + k * 4), scalar2=None, op0=ALU.mult),
                reads=[r_const, r_vt], writes=[r_dg[0]])
        for j in range(4):
            for k in range(3):
                P.op("dve", lambda e, j=j, k=k: e.tensor_scalar(
                    out=dg3[:, j, k, :], in0=ident_b[:], scalar1=vcol(V_WSH + k * 4 + j), scalar2=None, op0=ALU.mult),
                    reads=[r_const, r_vt], writes=[r_dg3])

        def ada_piece(p):
            slot, rsl = take()
            w = slot[:, 0:4096].rearrange("p (k c) -> p k c", k=8)
            mt, mr = mmr.next()
            for kc in range(8):
                P.op("pe", lambda e, kc=kc: e.matmul(mt[:], lhsT=cact_m[:, kc, :], rhs=w[:, kc, :],
                                                    start=(kc == 0), stop=(kc == 7)),
                     reads=[r_cact, rsl], writes=[mr])
            for q in range(4):
                P.op("dve", lambda e, q=q: e.scalar_tensor_tensor(
                    out=adscr[:], in0=mt[:, q * 128:(q + 1) * 128], scalar=1.0, in1=ident_f[:],
                    op0=ALU.mult, op1=ALU.mult, accum_out=adsum[:, q:q + 1]),
                    reads=[mr, r_const], writes=[r_adscr])
            sec = p // 2
            P.op("dve", lambda e: e.tensor_tensor(out=modT[:, 4 * p:4 * p + 4], in0=adsum[:, 0:4],
                                                 in1=vt[:, V_BADA + 4 * p:V_BADA + 4 * p + 4], op=ALU.add),
                 reads=[r_adscr, r_vt], writes=[r_mod[sec]])

        def ada_derive_gs(l):
            sc = 1 + 3 * l
            grow = V_GPRE1 if l == 0 else V_GPRE2
            P.op("dve", lambda e: e.scalar_tensor_tensor(out=gs[:, 8 * l:8 * l + 8], in0=modT[:, 8 * sc:8 * sc + 8], scalar=1.0,
                                                        in1=vt[:, grow:grow + 8], op0=ALU.add, op1=ALU.mult),
                 reads=[r_mod[sc], r_vt], writes=[r_gs[l]])

        def ada_derive_gtg(l):
            gt = 2 + 3 * l
            grow = V_GPOST1 if l == 0 else V_GPOST2
            P.op("dve", lambda e: e.tensor_tensor(out=gtg[:, 8 * l:8 * l + 8], in0=modT[:, 8 * gt:8 * gt + 8],
                                                 in1=vt[:, grow:grow + 8], op=ALU.mult),
                 reads=[r_mod[gt], r_vt], writes=[r_gtg[l]])

        def rsqrt_ops(src_ap, src_regs, dst_ap, dst_regs, scale=1.0):
            P.op("act", lambda e: e.activation(out=dst_ap, in_=src_ap, func=AF.Ln, bias=eps_t[:], scale=scale),
                 reads=list(src_regs) + [r_const], writes=list(dst_regs))
            P.op("act", lambda e: e.activation(out=dst_ap, in_=dst_ap, func=AF.Exp, scale=-0.5),
                 reads=list(dst_regs), writes=list(dst_regs))

        def prenorm_steps(T, l, s):
            sh = 0 if l == 0 else 3
            xns = []

            def st_stats():
                for ii in range(4):
                    i = s * 4 + ii
                    P.op("act", lambda e, i=i: e.activation(out=junk[:], in_=xb[:, i, :], func=AF.Square,
                                                           accum_out=ssq[:, i:i + 1]),
                         reads=[r_xb[i]], writes=[r_junk, r_ssq[s]])
                rsqrt_ops(ssq[:, 4 * s:4 * s + 4], [r_ssq[s]], rstd[:, 4 * s:4 * s + 4], [r_rstd[s]], scale=1.0 / D)

            def st_xn():
                for ii in range(4):
                    i = s * 4 + ii
                    xn, xnr = txn.next()
                    xns.append((xn, xnr))
                    P.op("dve", lambda e, i=i, xn=xn: e.tensor_scalar(out=xn[:], in0=xb[:, i, :], scalar1=rstd[:, i:i + 1],
                                                                   scalar2=None, op0=ALU.mult),
                         reads=[r_xb[i], r_rstd[s]], writes=[xnr])

            def st_half(half):
                tps = [tpr.next(), tpr.next()]
                tviews = [t[0].bitcast(BF16) for t in tps]
                for ii in range(4):
                    xn, xnr = xns[ii]
                    for kq in range(4):
                        kc = half * 4 + kq
                        tv = tviews[kq // 2]
                        c0 = (kq % 2) * 512 + ii * 128
                        P.op("pe", lambda e, kc=kc, tv=tv, c0=c0, xn=xn: e.transpose(
                            out=tv[:, c0:c0 + 128], in_=xn[:, kc * 128:(kc + 1) * 128], identity=ident_b[:]),
                            reads=[xnr, r_const], writes=[tps[kq // 2][1]])
                for kq in range(4):
                    kc = half * 4 + kq
                    tv = tviews[kq // 2]
                    c0 = (kq % 2) * 512
                    if kq // 2 == 0:
                        P.op("act", lambda e, kc=kc, tv=tv, c0=c0: e.activation(
                            out=hT[:, kc, s * 512:(s + 1) * 512], in_=tv[:, c0:c0 + 512], func=AF.Identity,
                            bias=modT[:, 8 * sh + kc:8 * sh + kc + 1], scale=gs[:, 8 * l + kc:8 * l + kc + 1]),
                            reads=[tps[kq // 2][1], r_mod[sh], r_gs[l]], writes=[r_hT[kc][s]])
                    else:
                        P.op("dve", lambda e, kc=kc, tv=tv, c0=c0: e.tensor_scalar(
                            out=hT[:, kc, s * 512:(s + 1) * 512], in0=tv[:, c0:c0 + 512],
                            scalar1=gs[:, 8 * l + kc:8 * l + kc + 1], scalar2=modT[:, 8 * sh + kc:8 * sh + kc + 1],
                            op0=ALU.mult, op1=ALU.add),
                            reads=[tps[kq // 2][1], r_mod[sh], r_gs[l]], writes=[r_hT[kc][s]])

            return [st_stats, st_xn, lambda: st_half(0), lambda: st_half(1)]

        def prenorm_part(T, l, s):
            for th in prenorm_steps(T, l, s):
                th()

        def _old_prenorm_part(T, l, s):
            sh = 0 if l == 0 else 3
            for ii in range(4):
                i = s * 4 + ii
                P.op("act", lambda e, i=i: e.activation(out=junk[:], in_=xb[:, i, :], func=AF.Square,
                                                       accum_out=ssq[:, i:i + 1]),
                     reads=[r_xb[i]], writes=[r_junk, r_ssq[s]])
            rsqrt_ops(ssq[:, 4 * s:4 * s + 4], [r_ssq[s]], rstd[:, 4 * s:4 * s + 4], [r_rstd[s]], scale=1.0 / D)
            xns = []
            for ii in range(4):
                i = s * 4 + ii
                xn, xnr = txn.next()
                xns.append((xn, xnr))
                P.op("dve", lambda e, i=i, xn=xn: e.tensor_scalar(out=xn[:], in0=xb[:, i, :], scalar1=rstd[:, i:i + 1],
                                                               scalar2=None, op0=ALU.mult),
                     reads=[r_xb[i], r_rstd[s]], writes=[xnr])
            for half in range(2):
                tps = [tpr.next(), tpr.next()]
                tviews = [t[0].bitcast(BF16) for t in tps]
                for ii in range(4):
                    xn, xnr = xns[ii]
                    for kq in range(4):
                        kc = half * 4 + kq
                        tv = tviews[kq // 2]
                        c0 = (kq % 2) * 512 + ii * 128
                        P.op("pe", lambda e, kc=kc, tv=tv, c0=c0, xn=xn: e.transpose(
                            out=tv[:, c0:c0 + 128], in_=xn[:, kc * 128:(kc + 1) * 128], identity=ident_b[:]),
                            reads=[xnr, r_const], writes=[tps[kq // 2][1]])
                for kq in range(4):
                    kc = half * 4 + kq
                    tv = tviews[kq // 2]
                    c0 = (kq % 2) * 512
                    if kq // 2 == 0:
                        P.op("act", lambda e, kc=kc, tv=tv, c0=c0, s=s: e.activation(
                            out=hT[:, kc, s * 512:(s + 1) * 512], in_=tv[:, c0:c0 + 512], func=AF.Identity,
                            bias=modT[:, 8 * sh + kc:8 * sh + kc + 1], scale=gs[:, 8 * l + kc:8 * l + kc + 1]),
                            reads=[tps[kq // 2][1], r_mod[sh], r_gs[l]], writes=[r_hT[kc][s]])
                    else:
                        P.op("dve", lambda e, kc=kc, tv=tv, c0=c0, s=s: e.tensor_scalar(
                            out=hT[:, kc, s * 512:(s + 1) * 512], in0=tv[:, c0:c0 + 512],
                            scalar1=gs[:, 8 * l + kc:8 * l + kc + 1], scalar2=modT[:, 8 * sh + kc:8 * sh + kc + 1],
                            op0=ALU.mult, op1=ALU.add),
                            reads=[tps[kq // 2][1], r_mod[sh], r_gs[l]], writes=[r_hT[kc][s]])

        def prenorm(T, l):
            for s in range(NS):
                prenorm_part(T, l, s)

        def mm_group(out_t, out_r, lhs_fn, rhs_fn, nk, reads):
            for k in range(nk):
                P.op("pe", lambda e, k=k: e.matmul(out_t[:], lhsT=lhs_fn(k), rhs=rhs_fn(k), start=(k == 0), stop=(k == nk - 1)),
                     reads=reads, writes=[out_r])

        def head_norm(yf, yfr, c, s, ring=None, fpool=None):
            sq, sqr = tb.next()
            P.op("act", lambda e: e.activation(out=sq[:], in_=yf[:], func=AF.Square), reads=[yfr], writes=[sqr])
            st_t, st_r = (ring or strr).next()
            P.op("pe", lambda e: e.matmul(st_t[:], lhsT=blk[:], rhs=sq[:], start=True, stop=True),
                 reads=[sqr, r_const], writes=[st_r])
            rr, rrr = (fpool or tf).next()
            rsqrt_ops(st_t[:], [st_r], rr[:], [rrr])
            P.op("dve", lambda e: e.scalar_tensor_tensor(out=yT[:, c, s * 512:(s + 1) * 512], in0=yf[:],
                                                        scalar=vcol(V_BETA + c), in1=rr[:], op0=ALU.mult, op1=ALU.mult),
                 reads=[yfr, rrr, r_vt], writes=[r_yT[c][s]])

        def mixer(T):
            for j in range(4):
                P.op("dve", lambda e, j=j: e.tensor_copy(out=cvb[:, j, 0:2], in_=cvh_s[:, j, :]), reads=[r_cvh[j]], writes=[r_cvhA[j]])
                P.op("dve", lambda e, j=j: e.tensor_copy(out=u0b[:, j, 0:30], in_=u0h_s[:, j, :]), reads=[r_u0h[j]], writes=[r_u0hA[j]])
            st8 = {}

            def proj(w, rsl, c, s):
                mt, mr = mmr.next()
                mm_group(mt, mr, lambda k: w[:, c, k, :], lambda k: hT[:, k, s * 512:(s + 1) * 512], 8,
                         [rsl] + [r_hT[k][s] for k in range(8)])
                return mt, mr

            def front_c(j, s):
                if s == 0:
                    slot, rsl = take()
                    st8["w"] = slot[:, 0:2048].rearrange("p (c k m) -> p c k m", c=2, k=8)
                    st8["rsl"] = rsl
                    db = (T * 4 + j) % 2
                    st8["db"] = db
                    for k in range(31):
                        if T == 0 and j == 0:
                            break
                        P.op("dve", lambda e, k=k: e.tensor_scalar(
                            out=dg[:, db, k, :], in0=ident_b[:], scalar1=vcol(V_WCF + k * 4 + j), scalar2=None, op0=ALU.mult),
                            reads=[r_const, r_vt], writes=[r_dg[db]])
                w, rsl = st8["w"], st8["rsl"]
                c0 = s * 512
                gt_, gr = proj(w, rsl, 0, s)
                at, ar = proj(w, rsl, 1, s)
                sg, sgr = tf.next()
                P.op("act", lambda e: e.activation(out=sg[:], in_=gt_[:], func=AF.Exp, scale=-1.0), reads=[gr], writes=[sgr])
                P.op("act", lambda e: e.activation(out=sg[:], in_=sg[:], func=AF.Ln, bias=one_t[:], scale=1.0), reads=[sgr, r_const], writes=[sgr])
                P.op("act", lambda e: e.activation(out=sg[:], in_=sg[:], func=AF.Exp, scale=-1.0), reads=[sgr], writes=[sgr])
                P.op("dve", lambda e: e.tensor_tensor(out=u0b[:, j, 30 + c0:30 + c0 + 512], in0=sg[:], in1=at[:], op=ALU.mult),
                     reads=[sgr, ar], writes=[r_u0b[j][s]])
                if T == 0 and s == NS - 1:
                    ada_piece(4 + j)
                return (st8["db"],)

            def back_c(j, s, db):
                c0 = s * 512
                prev_u0 = r_u0b[j][s - 1] if s > 0 else r_u0hA[j]
                ct2, cr2 = mmr.next()
                mm_group(ct2, cr2, lambda k: dg[:, db, k, :], lambda k: u0b[:, j, c0 + k:c0 + k + 512], 31,
                         [r_dg[db], r_u0b[j][s], prev_u0])
                P.op("act", lambda e: e.activation(out=u1[:, j, c0:c0 + 512], in_=ct2[:], func=AF.Identity,
                                                   bias=vcol(V_BCF + j), scale=1.0),
                     reads=[cr2, r_vt], writes=[r_u1[j][s]])
                if s == NS - 1 and T + 1 < NB:
                    P.op("dve", lambda e: e.tensor_copy(out=u0h_s[:, j, :], in_=u0b[:, j, TB:TB + 30]),
                         reads=[r_u0b[j][NS - 1]], writes=[r_u0h[j]])

            pend = None
            for j in range(4):
                for s in range(NS):
                    cur = (j, s) + front_c(j, s)
                    if pend is not None:
                        back_c(*pend)
                    pend = cur
            back_c(*pend)

            tfs = Pool(tf.bufs[8:10])
            def front_s(j, s):
                if s == 0:
                    slot, rsl = take()
                    st8["w"] = slot[:, 0:3072].rearrange("p (c k m) -> p c k m", c=3, k=8)
                    st8["rsl"] = rsl
                w, rsl = st8["w"], st8["rsl"]
                c0 = s * 512
                gct, gcr = proj(w, rsl, 0, s)
                vtt, vr = proj(w, rsl, 1, s)
                gcs, gcsr = tfs.next()
                P.op("act", lambda e: e.copy(out=gcs[:], in_=gct[:]), reads=[gcr], writes=[gcsr])
                P.op("dve", lambda e: e.tensor_tensor(out=cvb[:, j, 2 + c0:2 + c0 + 512], in0=gcs[:], in1=vtt[:], op=ALU.mult),
                     reads=[gcsr, vr], writes=[r_cvb[j][s]])
                gbt, gbr = proj(w, rsl, 2, s)
                gbs, gbsr = tgb.next()
                P.op("act", lambda e: e.copy(out=gbs[:], in_=gbt[:]), reads=[gbr], writes=[gbsr])
                if T == 0 and s == NS - 1:
                    ada_piece(8 + j)
                return (gbs, gbsr)

            ysb = [tf.bufs[8], tf.bufs[9]]
            sqb = [tb2.bufs[2], tb2.bufs[3]]

            def back_a(n, j, s, gbs, gbsr):
                c0 = s * 512
                prev_cv = r_cvb[j][s - 1] if s > 0 else r_cvhA[j]
                ct, cr = mmr.next()
                mm_group(ct, cr, lambda k: dg3[:, j, k, :], lambda k: cvb[:, j, c0 + k:c0 + k + 512], 3,
                         [r_dg3, r_cvb[j][s], prev_cv])
                ysf, ysr = ysb[n % 2]
                P.op("dve", lambda e: e.scalar_tensor_tensor(
                    out=ysf[:], in0=ct[:], scalar=vcol(V_BSH + j), in1=gbs[:], op0=ALU.add, op1=ALU.mult),
                    reads=[cr, gbsr, r_vt], writes=[ysr])
                sq, sqr = sqb[n % 2]
                P.op("act", lambda e: e.activation(out=sq[:], in_=ysf[:], func=AF.Square), reads=[ysr], writes=[sqr])
                if s == NS - 1 and T + 1 < NB:
                    P.op("dve", lambda e: e.tensor_copy(out=cvh_s[:, j, :], in_=cvb[:, j, TB:TB + 2]),
                         reads=[r_cvb[j][NS - 1]], writes=[r_cvh[j]])

            def back_b(n, j, s):
                ysf, ysr = ysb[n % 2]
                sq, sqr = sqb[n % 2]
                st_t, st_r = mmr.next()
                P.op("pe", lambda e: e.matmul(st_t[:], lhsT=blk[:], rhs=sq[:], start=True, stop=True),
                     reads=[sqr, r_const], writes=[st_r])
                rr, rrr = txp.next()
                rsqrt_ops(st_t[:], [st_r], rr[:], [rrr])
                P.op("dve", lambda e: e.scalar_tensor_tensor(out=yT[:, j, s * 512:(s + 1) * 512], in0=ysf[:],
                                                            scalar=vcol(V_BETA + j), in1=rr[:], op0=ALU.mult, op1=ALU.mult),
                     reads=[ysr, rrr, r_vt], writes=[r_yT[j][s]])

            short_steps = []
            sst = {"pa": None, "pb": None, "n": 0}

            def sstep_parts(j, s):
                box = {}
                c0 = s * 512

                def half(c, lo, hi, post=None):
                    def th():
                        if c == 0 and lo == 0 and s == 0:
                            slot, rsl = take()
                            st8["w"] = slot[:, 0:3072].rearrange("p (c k m) -> p c k m", c=3, k=8)
                            st8["rsl"] = rsl
                        w, rsl = st8["w"], st8["rsl"]
                        if lo == 0:
                            box[c] = mmr.next()
                        mt, mr = box[c]
                        for k in range(lo, hi):
                            P.op("pe", lambda e, k=k: e.matmul(mt[:], lhsT=w[:, c, k, :], rhs=hT[:, k, s * 512:(s + 1) * 512],
                                                               start=(k == 0), stop=(k == 7)),
                                 reads=[rsl] + [r_hT[kk][s] for kk in range(8)], writes=[mr])
                        if post is not None:
                            post()
                    return th

                def post_v():
                    (gct, gcr), (vtt, vr) = box[0], box[1]
                    gcs, gcsr = txp.next()
                    P.op("act", lambda e: e.copy(out=gcs[:], in_=gct[:]), reads=[gcr], writes=[gcsr])
                    P.op("dve", lambda e: e.tensor_tensor(out=cvb[:, j, 2 + c0:2 + c0 + 512], in0=gcs[:], in1=vtt[:], op=ALU.mult),
                         reads=[gcsr, vr], writes=[r_cvb[j][s]])

                def post_gb():
                    gbt, gbr = box[2]
                    gbs, gbsr = tgb.next()
                    P.op("act", lambda e: e.copy(out=gbs[:], in_=gbt[:]), reads=[gbr], writes=[gbsr])
                    if T == 0 and s == NS - 1:
                        ada_piece(8 + j)
                    if sst["pa"] is not None:
                        back_a(*sst["pa"])
                    if sst["pb"] is not None:
                        back_b(*sst["pb"])
                    sst["pb"] = sst["pa"][0:3] if sst["pa"] is not None else None
                    sst["pa"] = (sst["n"], j, s, gbs, gbsr)
                    sst["n"] += 1

                return [half(0, 0, 4), half(0, 4, 8), half(1, 0, 4), half(1, 4, 8, post_v),
                        half(2, 0, 4), half(2, 4, 8, post_gb)]

            for j in range(4):
                for s in range(NS):
                    short_steps.extend(sstep_parts(j, s))
            def sflush():
                back_a(*sst["pa"])
                if sst["pb"] is not None:
                    back_b(*sst["pb"])
                back_b(*sst["pa"][0:3])
            short_steps.append(sflush)

            def ln_rows(s):
                c0 = s * 512
                if s == 0:
                    (mean_t, mean_r), (msq_t, msq_r) = stb[0], stb[1]
                else:
                    (mean_t, mean_r), (msq_t, msq_r) = tpr.next(), tpr.next()
                ubs = [tb.next() for _ in range(4)] if s == 0 else [tb2.bufs[j % 2] for j in range(4)]
                t1s = [(u1[:, j, c0:c0 + 512], r_u1[j][s]) for j in range(4)]
                ezs = [tf.bufs[4 * s + j] for j in range(4)]
                m2, m2r = ezs[0]
                rows = []

                def per_j(fn):
                    rows.append([(lambda j=j: fn(j)) for j in range(4)])

                def cast_p(j):
                    ub, ubr = ubs[j]
                    P.op("dve", lambda e: e.tensor_copy(out=ub[:], in_=u1[:, j, c0:c0 + 512]), reads=[r_u1[j][s]], writes=[ubr])

                def cast_c(j):
                    ub, ubr = ubs[j]
                    P.op("pe", lambda e: e.matmul(mean_t[:], lhsT=ones_ln[:], rhs=ub[:], start=(j == 0), stop=(j == 3)),
                         reads=[ubr, r_const], writes=[mean_r])

                def sq_p(j):
                    us, usr = ubs[j]
                    P.op("act", lambda e: e.activation(out=us[:], in_=u1[:, j, c0:c0 + 512], func=AF.Square), reads=[r_u1[j][s]], writes=[usr])

                def sq_c(j):
                    us, usr = ubs[j]
                    P.op("pe", lambda e: e.matmul(msq_t[:], lhsT=ones_ln[:], rhs=us[:], start=(j == 0), stop=(j == 3)),
                         reads=[usr, r_const], writes=[msq_r])

                rows.append([lambda: cast_p(0)] + [(lambda j=j: (cast_p(j), cast_c(j - 1))) for j in range(1, 4)])
                rows.append([lambda: (cast_c(3), sq_p(0))] + [(lambda j=j: (sq_p(j), sq_c(j - 1))) for j in range(1, 4)])

                def r_var():
                    P.op("act", lambda e: e.activation(out=m2[:], in_=mean_t[:], func=AF.Square), reads=[mean_r], writes=[m2r])
                    P.op("dve", lambda e: e.tensor_tensor(out=m2[:], in0=msq_t[:], in1=m2[:], op=ALU.subtract), reads=[msq_r, m2r], writes=[m2r])
                    rsqrt_ops(m2[:], [m2r], msq_t[:], [msq_r])
                rows.append([lambda: (sq_c(3), r_var())])

                def sub(j):
                    t1, t1r = t1s[j]
                    P.op("dve", lambda e: e.tensor_tensor(out=t1, in0=t1, in1=mean_t[:], op=ALU.subtract), reads=[t1r, mean_r], writes=[t1r])
                per_j(sub)

                def mul(j):
                    t1, t1r = t1s[j]
                    P.op("dve", lambda e: e.tensor_tensor(out=t1, in0=t1, in1=msq_t[:], op=ALU.mult), reads=[t1r, msq_r], writes=[t1r])
                per_j(mul)

                def e1(j):
                    t1, t1r = t1s[j]
                    ez, ezr = ezs[j]
                    P.op("act", lambda e: e.activation(out=ez[:], in_=t1, func=AF.Exp, bias=nvt[:, 4 + j:5 + j], scale=nvt[:, j:j + 1]),
                         reads=[t1r, r_vt], writes=[ezr])
                per_j(e1)

                def ln1(j):
                    t1, t1r = t1s[j]
                    ez, ezr = ezs[j]
                    P.op("act", lambda e: e.activation(out=ez[:], in_=ez[:], func=AF.Ln, bias=one_t[:], scale=1.0), reads=[ezr, r_const], writes=[ezr])
                    P.op("dve", lambda e: e.tensor_scalar(out=t1, in0=t1, scalar1=vcol(V_GLN + j), scalar2=vcol(V_BLN + j),
                                                          op0=ALU.mult, op1=ALU.add),
                         reads=[t1r, r_vt], writes=[t1r])
                per_j(ln1)

                def e2(j):
                    ez, ezr = ezs[j]
                    P.op("act", lambda e: e.activation(out=ez[:], in_=ez[:], func=AF.Exp, scale=-1.0), reads=[ezr], writes=[ezr])
                per_j(e2)

                def silu(j):
                    t1, t1r = t1s[j]
                    ez, ezr = ezs[j]
                    P.op("dve", lambda e: e.tensor_tensor(out=t1, in0=t1, in1=ez[:], op=ALU.mult), reads=[t1r, ezr], writes=[t1r])
                per_j(silu)

                def hsq_p(j):
                    t1, t1r = t1s[j]
                    sq_, sqr = ubs[j]
                    P.op("act", lambda e: e.activation(out=sq_[:], in_=t1, func=AF.Square), reads=[t1r], writes=[sqr])

                def hsq_c(j):
                    sq_, sqr = ubs[j]
                    P.op("pe", lambda e: e.matmul(mean_t[:], lhsT=blk[:], rhs=sq_[:], start=True, stop=True),
                         reads=[sqr, r_const], writes=[mean_r])
                    ez, ezr = ezs[j]
                    P.op("act", lambda e: e.activation(out=ez[:], in_=mean_t[:], func=AF.Ln, bias=eps_t[:], scale=1.0),
                         reads=[mean_r, r_const], writes=[ezr])

                def hexp(j):
                    ez, ezr = ezs[j]
                    P.op("act", lambda e: e.activation(out=ez[:], in_=ez[:], func=AF.Exp, scale=-0.5), reads=[ezr], writes=[ezr])

                rows.append([lambda: hsq_p(0)] + [(lambda j=j: (hsq_p(j), hsq_c(j - 1))) for j in range(1, 4)])
                rows.append([lambda: (hsq_c(3), hexp(0))] + [(lambda j=j: hexp(j)) for j in range(1, 4)])

                def out(j):
                    t1, t1r = t1s[j]
                    ez, ezr = ezs[j]
                    P.op("dve", lambda e: e.scalar_tensor_tensor(
                        out=yT[:, 4 + j, s * 512:(s + 1) * 512], in0=t1, scalar=vcol(V_BETA + 4 + j), in1=ez[:],
                        op0=ALU.mult, op1=ALU.mult),
                        reads=[t1r, ezr, r_vt], writes=[r_yT[4 + j][s]])
                per_j(out)
                return rows

            rows0 = ln_rows(0)
            rows1 = ln_rows(1)
            LAG = 0
            ln_list = []
            for k in range(len(rows0) + LAG):
                ra = rows0[k] if k < len(rows0) else []
                rb = rows1[k - LAG] if 0 <= k - LAG < len(rows1) else []
                for i in range(max(len(ra), len(rb))):
                    if i < len(ra):
                        ln_list.append(ra[i])
                    if i < len(rb):
                        ln_list.append(rb[i])
            interleave(ln_list + [(lambda: None)] * 24, short_steps)

        def out_proj_parts(T, l, srcT, r_src, nk, fT, r_fT):
            ssq_b = [stb[s] for s in range(NS)]
            st = {"pend": None}

            def flush():
                if st["pend"] is not None:
                    st["pend"]()
                    st["pend"] = None

            def main(m, s, lhs, rsl):
                mt, mr = mmr.next()
                mm_group(mt, mr, lhs, lambda k: srcT[:, k, s * 512:(s + 1) * 512], nk,
                         [rsl] + [r_src[k][s] for k in range(nk)])
                P.op("act", lambda e: e.copy(out=fT(m, s), in_=mt[:]), reads=[mr], writes=[r_fT[m][s]])
                sq, sqr = tb.next()
                P.op("act", lambda e: e.activation(out=sq[:], in_=mt[:], func=AF.Square), reads=[mr], writes=[sqr])
                flush()
                st["pend"] = (lambda: P.op(
                    "pe", lambda e: e.matmul(ssq_b[s][0][:], lhsT=ones_d[:], rhs=sq[:], start=(m == 0), stop=(m == 7)),
                    reads=[sqr, r_const], writes=[ssq_b[s][1]]))

            def main_parts(m, s, lhs, rsl, parts=2):
                box = {}
                per = nk // parts
                ths = []
                for p in range(parts):
                    lo, hi = p * per, (nk if p == parts - 1 else (p + 1) * per)

                    def th(lo=lo, hi=hi, p=p):
                        if p == 0:
                            box["mt"], box["mr"] = mmr.next()
                        mt, mr = box["mt"], box["mr"]
                        for k in range(lo, hi):
                            P.op("pe", lambda e, k=k: e.matmul(mt[:], lhsT=lhs(k), rhs=srcT[:, k, s * 512:(s + 1) * 512],
                                                               start=(k == 0), stop=(k == nk - 1)),
                                 reads=[rsl] + [r_src[kk][s] for kk in range(nk)], writes=[mr])
                        if p == parts - 1:
                            P.op("act", lambda e: e.copy(out=fT(m, s), in_=mt[:]), reads=[mr], writes=[r_fT[m][s]])
                            sq, sqr = tb.next()
                            P.op("act", lambda e: e.activation(out=sq[:], in_=mt[:], func=AF.Square), reads=[mr], writes=[sqr])
                            flush()
                            st["pend"] = (lambda: P.op(
                                "pe", lambda e: e.matmul(ssq_b[s][0][:], lhsT=ones_d[:], rhs=sq[:], start=(m == 0), stop=(m == 7)),
                                reads=[sqr, r_const], writes=[ssq_b[s][1]]))
                    ths.append(th)
                return ths

            def tail(s):
                th = []
                th.append(lambda: rsqrt_ops(ssq_b[s][0][:], [ssq_b[s][1]], ssq_b[s][0][:], [ssq_b[s][1]]))
                tgp = [tf.bufs[0], tf.bufs[1]]
                tst = {"T": None, "A": None}

                def do_T(m):
                    tg, tgr = tgp[m % 2]
                    tp_t, tp_r = tpr.next()
                    for ii in range(4):
                        P.op("pe", lambda e, ii=ii: e.transpose(
                            out=tp_t[:, ii * 128:(ii + 1) * 128], in_=tg[:, ii * 128:(ii + 1) * 128], identity=ident_f[:]),
                            reads=[tgr, r_const], writes=[tp_r])
                    xv = xb[:, s * 4:s * 4 + 4, m * 128:(m + 1) * 128]
                    return (lambda: P.op(
                        "dve", lambda e: e.tensor_tensor(out=xv, in0=xv, in1=tp_t[:].rearrange("p (i c) -> p i c", i=4), op=ALU.add),
                        reads=[tp_r] + [r_xb[s * 4 + ii] for ii in range(4)], writes=[r_xb[s * 4 + ii] for ii in range(4)]))

                def advance():
                    newA = tst["T"]() if tst["T"] is not None else None
                    if tst["A"] is not None:
                        tst["A"]()
                    tst["A"] = newA
                    tst["T"] = None

                def step(m):
                    tg, tgr = tgp[m % 2]
                    P.op("dve", lambda e: e.scalar_tensor_tensor(
                        out=tg[:], in0=fT(m, s), scalar=gtg[:, 8 * l + m:8 * l + m + 1],
                        in1=ssq_b[s][0][:], op0=ALU.mult, op1=ALU.mult),
                        reads=[r_fT[m][s], r_gtg[l], ssq_b[s][1]], writes=[tgr])
                    advance()
                    tst["T"] = (lambda: do_T(m))

                for m in range(8):
                    th.append(lambda m=m: step(m))

                def fin():
                    advance()
                    advance()
                th.append(fin)
                return th

            return main, flush, tail, main_parts

        def ffn_group(w, rsl, g, jj, s):
            j = 2 * g + jj
            gt_, gr = mmr.next()
            mm_group(gt_, gr, lambda k: w[:, jj, 0, k, :], lambda k: hT[:, k, s * 512:(s + 1) * 512], 8,
                     [rsl] + [r_hT[k][s] for k in range(8)])
            ut, ur = mmr.next()
            mm_group(ut, ur, lambda k: w[:, jj, 1, k, :], lambda k: hT[:, k, s * 512:(s + 1) * 512], 8,
                     [rsl] + [r_hT[k][s] for k in range(8)])
            sl, slr = tff.next()
            P.op("act", lambda e: e.activation(out=sl[:], in_=gt_[:], func=AF.Silu), reads=[gr], writes=[slr])
            P.op("dve", lambda e: e.tensor_tensor(out=actT[:, j, s * 512:(s + 1) * 512], in0=sl[:], in1=ut[:], op=ALU.mult),
                 reads=[slr, ur], writes=[r_act[j][s]])

        def ffn_group_parts(w, rsl, g, jj, s):
            j = 2 * g + jj
            box = {}
            rd = [rsl] + [r_hT[k][s] for k in range(8)]

            def half(u, lo, hi, last=False):
                def th():
                    if lo == 0:
                        box[u] = mmr.next()
                    t_, r_ = box[u]
                    for k in range(lo, hi):
                        P.op("pe", lambda e, k=k: e.matmul(t_[:], lhsT=w[:, jj, u, k, :], rhs=hT[:, k, s * 512:(s + 1) * 512],
                                                           start=(k == 0), stop=(k == 7)),
                             reads=rd, writes=[r_])
                    if last:
                        (gt_, gr), (ut, ur) = box[0], box[1]
                        sl, slr = tff.next()
                        P.op("act", lambda e: e.activation(out=sl[:], in_=gt_[:], func=AF.Silu), reads=[gr], writes=[slr])
                        P.op("dve", lambda e: e.tensor_tensor(out=actT[:, j, s * 512:(s + 1) * 512], in0=sl[:], in1=ut[:], op=ALU.mult),
                             reads=[slr, ur], writes=[r_act[j][s]])
                return th
            return [half(0, 0, 4), half(0, 4, 8), half(1, 0, 4), half(1, 4, 8, last=True)]

        def interleave(a, b):
            na, nb = len(a), len(b)
            ia = ib = 0
            while ia < na or ib < nb:
                if ib >= nb or (ia < na and ia * nb <= ib * na):
                    a[ia]()
                    ia += 1
                else:
                    b[ib]()
                    ib += 1

        def load_x(T, tiles=None, after=()):
            for i in (range(NT) if tiles is None else tiles):
                r0 = (T * NT + i) * 128
                P.op("sp", lambda e, i=i, r0=r0: e.dma_start(out=xb[:, i, :], in_=x_d[r0:r0 + 128, :]),
                     reads=list(after), writes=[r_xb[i]], dsem=ds_x[i])

        def store_out(T):
            for i in range(NT):
                r0 = (T * NT + i) * 128
                P.op("sp", lambda e, i=i, r0=r0: e.dma_start(out=out_d[r0:r0 + 128, :], in_=xb[:, i, :]),
                     reads=[r_xb[i]], dsem=ds_o[i])

        try:
            load_x(0, tiles=range(0, 4))
            if STOP["at"] == "load":
                raise _StopBuild()
            for p in range(4):
                ada_piece(p)
            load_x(0, tiles=range(4, NT))
            ada_derive_gs(0)
            if STOP["at"] == "ada0":
                raise _StopBuild()
            prenorm(0, 0)
            for T in range(NB):
                if STOP["at"] == "prenorm0":
                    raise _StopBuild()
                if T == 0:
                    dump("h1T", hT[:], [128, 8, TB], BF16, [r for k in r_hT for r in k])
                mixer(T)
                if STOP["at"] == "mixer":
                    raise _StopBuild()
                if T == 0:
                    dump("yT", yT[:], [128, 8, TB], BF16, [r for k in r_yT for r in k])
                    ada_derive_gtg(0)
                    ada_derive_gs(1)
                    ada_derive_gtg(1)
                P.fence(flat(r_u1) + flat(r_cvb) + flat(r_u0b) + r_cvhA + r_u0hA, flat(r_fTC))
                mainC, flushC, tailC, mainC_parts = out_proj_parts(T, 0, yT, r_yT, 8, fT_C, r_fTC)
                pcs = [take(hold=True), take(hold=True)]

                def lhsC(m, pcs=pcs):
                    slot, rsl = pcs[m // 4]
                    w = slot[:, 0:4096].rearrange("p (m k c) -> p m k c", m=4, k=8)
                    return (lambda k: w[:, m % 4, k, :]), rsl

                for m in range(8):
                    mainC(m, 0, *lhsC(m))
                flushC()
                A = tailC(0) + prenorm_steps(T, 1, 0)
                B = [t for m in range(8) for t in mainC_parts(m, 1, *lhsC(m))] + [flushC]
                interleave(A, B)
                end_hold()
                NE = 3
                P.fence([r_fTC[m][0] for m in range(8)], [r_act[j][ss] for j in range(8) for ss in range(NS)])
                fp = [take(hold=True) for _ in range(NE)]
                fw = [(sl[:, 0:4096].rearrange("p (j u k m) -> p j u k m", j=2, u=2, k=8), rs) for sl, rs in fp]
                A2 = tailC(1) + prenorm_steps(T, 1, 1)
                B2 = [t for g in range(NE) for jj in range(2) for t in ffn_group_parts(fw[g][0], fw[g][1], g, jj, 0)]
                interleave(A2, B2)
                if T == 0:
                    dump("x1", xb[:], [128, NT, D], F32, r_xb)
                P.fence(flat(r_fTC) + r_txn + [r_junk], [r_act[j][ss] for j in range(8, NJ) for ss in range(NS)])
                for g in range(NE):
                    for jj in range(2):
                        ffn_group(fw[g][0], fw[g][1], g, jj, 1)
                end_hold()
                for g in range(NE, 11):
                    slot, rsl = take()
                    w = slot[:, 0:4096].rearrange("p (j u k m) -> p j u k m", j=2, u=2, k=8)
                    for jj in range(2):
                        for ss in range(NS):
                            ffn_group(w, rsl, g, jj, ss)
                if T == 0:
                    dump("actT", actT[:], [128, NJ, TB], BF16, [r for k in r_act for r in k])
                P.fence(flat(r_hT) + flat(r_yT), flat(r_fTF))

                def after_f(s, T=T):
                    for ii in range(4):
                        i = 4 * s + ii
                        r0 = (T * NT + i) * 128
                        P.op("sp", lambda e, i=i, r0=r0: e.dma_start(out=out_d[r0:r0 + 128, :], in_=xb[:, i, :]),
                             reads=[r_xb[i]], dsem=ds_o[i])
                    if T + 1 < NB:
                        for ii in range(4):
                            i = 4 * s + ii
                            r0 = ((T + 1) * NT + i) * 128
                            P.op("sp", lambda e, i=i, r0=r0: e.dma_start(out=xb[:, i, :], in_=x_d[r0:r0 + 128, :]),
                                 writes=[r_xb[i]], dsem=ds_x[i])

                mainF, flushF, tailF, mainF_parts = out_proj_parts(T, 1, actT, r_act, NJ, fT_F, r_fTF)
                if T + 1 < NB:
                    for m in range(8):
                        slot, rsl = take()
                        w = slot[:, 0:2816].rearrange("p (j c) -> p j c", j=NJ)
                        for ss in range(NS):
                            mainF(m, ss, (lambda k, w=w: w[:, k, :]), rsl)
                    flushF()
                    for ss in range(NS):
                        for th in tailF(ss):
                            th()
                        after_f(ss)
                else:
                    def mF(m, ss):
                        slot, rsl = take()
                        w = slot[:, 0:2816].rearrange("p (j c) -> p j c", j=NJ)
                        mainF(m, ss, (lambda k, w=w: w[:, k, :]), rsl)

                    for m in range(8):
                        mF(m, 0)
                    flushF()
                    def mF_parts(m, ss):
                        box = {}

                        def first():
                            slot, rsl = take()
                            box["w"] = slot[:, 0:2816].rearrange("p (j c) -> p j c", j=NJ)
                            box["ths"] = mainF_parts(m, ss, (lambda k: box["w"][:, k, :]), rsl, parts=4)
                            box["ths"][0]()
                        return [first] + [(lambda i=i: box["ths"][i]()) for i in range(1, 4)]

                    interleave([t for m in range(8) for t in mF_parts(m, 1)] + [flushF],
                               tailF(0) + [lambda: after_f(0)])
                    for th in tailF(1):
                        th()
                    after_f(1)
                if T + 1 < NB:
                    P.fence(flat(r_act), r_txn + [r_junk] + flat(r_u1) + flat(r_cvb) + flat(r_u0b) + r_cvhA + r_u0hA)
                    P.fence(flat(r_fTF), flat(r_hT) + flat(r_yT))
                    for ss in range(NS):
                        prenorm_part(T + 1, 0, ss)

        except _StopBuild:
            store_out(0)
        if DEBUG.get("sbuf"):
            print("SBUF bytes remaining per partition:", nc.sbuf_bytes_remaining)
        P.emit(esems)
    return nc, dbg_d


def _layouts(inp):
    f = lambda a: np.ascontiguousarray(np.asarray(a, dtype=np.float32))
    w_ada = f(inp["w_ada"])[0]
    w_in = f(inp["w_in"])[0]
    w_out = f(inp["w_out"])[0]
    w_gu = f(inp["w_gate_up"])[0]
    w_dn = f(inp["w_down"])[0]
    wada = f(w_ada.reshape(8, 128, 12, 512).transpose(2, 1, 0, 3)).reshape(12, 128, 4096)
    t = w_in.reshape(8, 128, 5, 4, 128)[:, :, [1, 2, 0, 3, 4], :, :]
    tt = f(t.transpose(3, 1, 2, 0, 4))
    winc = f(tt[:, :, [4, 3]]).reshape(4, 128, 2048)
    wins = f(tt[:, :, 0:3]).reshape(4, 128, 3072)
    t = w_out.reshape(8, 128, 2, 4, 128)
    wout = f(t.transpose(2, 1, 3, 0, 4)).reshape(2, 128, 4096)
    t = w_gu.reshape(8, 128, 2, 11, 2, 128)
    wgu = f(t.transpose(3, 1, 4, 2, 0, 5)).reshape(11, 128, 4096)
    t = w_dn.reshape(NJ, 128, 8, 128)
    wdn = f(t.transpose(2, 1, 0, 3)).reshape(8, 128, 2816)
    return wada, winc, wins, wout, wgu, wdn


def _vmat(inp, b):
    rows = np.zeros((V_ROWS, 128), np.float32)

    def put(r0, a):
        a = np.asarray(a, dtype=np.float32).reshape(-1, 128)
        rows[r0:r0 + a.shape[0]] = a

    put(V_C, inp["c"][b])
    put(V_BADA, inp["b_ada"][0])
    put(V_GPRE1, inp["g_pre_mix"][0])
    put(V_GPOST1, inp["g_post_mix"][0])
    put(V_GPRE2, inp["g_pre_ffn"][0])
    put(V_GPOST2, inp["g_post_ffn"][0])
    put(V_BETA, inp["beta_mix"][0])
    put(V_WSH, inp["w_short"][0])
    put(V_BSH, inp["b_short"][0])
    put(V_WCF, inp["w_cfm_dw"][0])
    put(V_BCF, inp["b_cfm_dw"][0])
    put(V_GLN, inp["g_cfm_ln"][0])
    put(V_BLN, inp["b_cfm_ln"][0])
    return rows


_CACHE = {}


def kernel(**inputs):
    inp = {k: np.asarray(v) for k, v in inputs.items()}
    debug = bool(DEBUG.get("on"))
    key = ("nc", debug)
    if key not in _CACHE:
        _CACHE[key] = build_program(debug=debug)
    nc, dbg_d = _CACHE[key]
    wada, winc, wins, wout, wgu, wdn = _layouts(inp)
    x = np.asarray(inp["x"], dtype=np.float32)
    in_maps = []
    for b in range(NCORE):
        in_maps.append({
            "x": np.ascontiguousarray(x[b]),
            "vmat": _vmat(inp, b),
            "wada": wada, "winc": winc, "wins": wins, "wout": wout, "wgu": wgu, "wdn": wdn,
        })
    res = run_bass_kernel_spmd(nc, in_maps, core_ids=list(range(NCORE)))
    if debug:
        DEBUG["results"] = res.results
    out = np.stack([np.asarray(res.results[b]["out"], dtype=np.float32) for b in range(NCORE)], axis=0)
    return out
```

```python
import numpy as np
from contextlib import ExitStack
import concourse.bass as bass
import concourse.mybir as mybir
from concourse.bass_utils import run_bass_kernel_spmd

F32 = mybir.dt.float32
BF16 = mybir.dt.bfloat16
ALU = mybir.AluOpType
AF = mybir.ActivationFunctionType

D = 1024
S = 2048
NCORE = 8
NS = 2
TB = NS * 512
NB = S // TB
NT = TB // 128
HID = 2816
NJ = HID // 128
EPS = 1e-6
RING = 4
SLOT_F = 5120

V_C = 0
V_BADA = 8
V_GPRE1 = 56
V_GPOST1 = 64
V_GPRE2 = 72
V_GPOST2 = 80
V_BETA = 88
V_WSH = 96
V_BSH = 108
V_WCF = 112
V_BCF = 236
V_GLN = 240
V_BLN = 244
V_ROWS = 256

DEBUG = {}
STOP = {"at": None}


class _StopBuild(Exception):
    pass


class Reg:
    __slots__ = ("name", "w", "rs", "fence", "psum")

    def __init__(self, name, psum=False):
        self.name = name
        self.w = None
        self.rs = []
        self.fence = ()
        self.psum = psum


class DmaSem:
    __slots__ = ("sem", "count", "name")

    def __init__(self, sem, name):
        self.sem = sem
        self.count = 0
        self.name = name


class Op:
    __slots__ = ("eng", "calls", "deps", "sig", "val", "dsem", "dval", "name", "idx")


class _Rec:
    def __init__(self):
        self.calls = []

    def __getattr__(self, name):
        def f(*a, **kw):
            self.calls.append((name, a, kw))
            return self
        return f


class Prog:
    ENGS = ("pe", "act", "dve", "pool", "sp")

    def __init__(self, nc):
        self.nc = nc
        self.q = {e: [] for e in self.ENGS}
        self.n = 0
        self.dsems = []

    def reg(self, name, psum=False):
        return Reg(name, psum)

    def regs_n(self, name, n):
        return [Reg("%s%d" % (name, i)) for i in range(n)]

    def dsem(self, sem, name):
        d = DmaSem(sem, name)
        self.dsems.append(d)
        return d

    def fence(self, regs_old, regs_new):
        ops = []
        for r in regs_old:
            if r.w is not None:
                ops.append(r.w)
            ops.extend(r.rs)
            ops.extend(r.fence)
        last = {}
        keep = []
        for o in ops:
            if o.dsem is not None:
                keep.append(o)
            elif o.eng not in last or o.idx > last[o.eng].idx:
                last[o.eng] = o
        keep.extend(last.values())
        for r in regs_new:
            r.fence = tuple(keep)

    def op(self, eng, fn, reads=(), writes=(), dsem=None, name=""):
        o = Op()
        o.eng = eng
        rec = _Rec()
        fn(rec)
        o.calls = rec.calls
        o.sig = False
        o.val = None
        o.dsem = dsem
        o.name = name
        o.idx = self.n
        self.n += 1
        deps = {}
        for r in reads:
            if r.w is not None:
                deps[id(r.w)] = r.w
            for f in r.fence:
                deps[id(f)] = f
            if r.psum:
                for x in r.rs:
                    if x.eng != eng:
                        deps[id(x)] = x
        for r in writes:
            if r.w is not None:
                deps[id(r.w)] = r.w
            for x in r.rs:
                deps[id(x)] = x
            for f in r.fence:
                deps[id(f)] = f
        dl = []
        for d in deps.values():
            if d.dsem is not None:
                dl.append((d, d.dsem.count))
            else:
                if d.eng == "pe" and eng == "pe" and dsem is None:
                    continue
                d.sig = True
                dl.append((d, None))
        o.deps = dl
        if dsem is not None:
            dsem.count += 16
            o.dval = dsem.count
        else:
            o.dval = None
        for r in reads:
            r.rs.append(o)
        for r in writes:
            r.w = o
            r.rs = []
        self.q[eng].append(o)
        return o

    def emit(self, esems, final_eng="sp"):
        nc = self.nc
        for e in self.ENGS:
            c = 0
            for o in self.q[e]:
                if o.dsem is None and o.sig:
                    c += 1
                    o.val = c
        prog = self

        def run(e, eng):
            waited = {}
            for o in prog.q[e]:
                for d, dv in o.deps:
                    if d.dsem is not None:
                        sem, val = d.dsem.sem, dv
                    else:
                        sem, val = esems[d.eng], d.val
                    k = id(sem)
                    if waited.get(k, 0) < val:
                        eng.wait_ge(sem, val)
                        waited[k] = val
                ins = None
                for (mname, a, kw) in o.calls:
                    ins = getattr(eng, mname)(*a, **kw)
                if o.dsem is not None:
                    ins.then_inc(o.dsem.sem, 16)
                elif o.sig:
                    ins.then_inc(esems[e], 1)
            if e == final_eng:
                for d in prog.dsems:
                    if d.count > 0:
                        eng.wait_ge(d.sem, d.count)

        with nc.Block() as block:

            @block.tensor
            def _(eng):
                run("pe", eng)

            @block.scalar
            def _(eng):
                run("act", eng)

            @block.vector
            def _(eng):
                run("dve", eng)

            @block.gpsimd
            def _(eng):
                run("pool", eng)

            @block.sync
            def _(eng):
                run("sp", eng)


class Pool:
    def __init__(self, bufs):
        self.bufs = bufs
        self.i = 0

    def next(self):
        b = self.bufs[self.i % len(self.bufs)]
        self.i += 1
        return b


def build_program(debug=False):
    nc = bass.Bass("TRN2", target_bir_lowering=False)
    x_d = nc.dram_tensor("x", [S, D], F32, kind="ExternalInput").ap()
    v_d = nc.dram_tensor("vmat", [V_ROWS, 128], F32, kind="ExternalInput").ap()
    wada_d = nc.dram_tensor("wada", [12, 128, 4096], F32, kind="ExternalInput").ap()
    winc_d = nc.dram_tensor("winc", [4, 128, 2048], F32, kind="ExternalInput").ap()
    wins_d = nc.dram_tensor("wins", [4, 128, 3072], F32, kind="ExternalInput").ap()
    wout_d = nc.dram_tensor("wout", [2, 128, 4096], F32, kind="ExternalInput").ap()
    wgu_d = nc.dram_tensor("wgu", [11, 128, 4096], F32, kind="ExternalInput").ap()
    wdn_d = nc.dram_tensor("wdn", [8, 128, 2816], F32, kind="ExternalInput").ap()
    out_d = nc.dram_tensor("out", [S, D], F32, kind="ExternalOutput").ap()
    dbg_d = {}

    P = Prog(nc)
    es = ExitStack()
    with es:
        def sb(name, shape, dt):
            return es.enter_context(nc.sbuf_tensor(name, shape, dt))

        def ps(name):
            return es.enter_context(nc.psum_tensor(name, [128, 512], F32))

        def sem(name):
            return es.enter_context(nc.semaphore(name))

        esems = {e: sem("s_" + e) for e in ("pe", "act", "dve", "pool")}

        ident_f = sb("ident_f", [128, 128], F32)
        ident_b = sb("ident_b", [128, 128], BF16)
        ones_ln = sb("ones_ln", [128, 128], BF16)
        ones_d = sb("ones_d", [128, 128], BF16)
        blk = sb("blk", [128, 128], BF16)
        eps_t = sb("eps_t", [128, 1], F32)
        one_t = sb("one_t", [128, 1], F32)
        v_sb = sb("v_sb", [128, 2, 128], F32)
        vt = sb("vt", [128, V_ROWS], F32)
        nvt = sb("nvt", [128, 8], F32)
        ctmp = sb("ctmp", [128, 8], F32)
        modT = sb("modT", [128, 48], F32)
        gs = sb("gs", [128, 16], F32)
        gtg = sb("gtg", [128, 16], F32)
        cact_m = sb("cact_m", [128, 8, 128], BF16)
        onesb = sb("onesb", [128, 128], BF16)
        adscr = sb("adscr", [128, 128], F32)
        adsum = sb("adsum", [128, 4], F32)
        dg = sb("dg", [128, 2, 31, 128], BF16)
        dg3 = sb("dg3", [128, 4, 3, 128], BF16)
        ssq = sb("ssq", [128, NT], F32)
        rstd = sb("rstd", [128, NT], F32)

        r_const = P.reg("const")
        r_vsb = P.reg("v_sb")
        r_vt = P.reg("vt")
        r_cact = P.reg("cact")
        r_ctmp = P.reg("ctmp")
        r_mod = [P.reg("mod%d" % i) for i in range(6)]
        r_gs = [P.reg("gs0"), P.reg("gs1")]
        r_gtg = [P.reg("gtg0"), P.reg("gtg1")]
        r_adscr = P.reg("adscr")
        r_dg = [P.reg("dg0"), P.reg("dg1")]
        r_dg3 = P.reg("dg3")
        r_ssq = P.regs_n("ssq", NS)
        r_rstd = P.regs_n("rstd", NS)

        xb = sb("xb", [128, NT, D], F32)
        assert NS == 2
        hy = sb("hy", [128, 16, TB], BF16)
        hT = hy[:, 0:8, :]
        yT = hy[:, 8:16, :]
        hyf = hy[:].rearrange("p a t -> p (a t)").bitcast(F32).rearrange("p (s m t) -> p s m t", s=2, m=8)
        fT_F = lambda m, s: hyf[:, s, m, :]
        R = sb("R", [128, 11 * TB], F32)
        actT = R[:].bitcast(BF16).rearrange("p (j t) -> p j t", j=NJ)
        u1 = R[:, 0:4 * TB].rearrange("p (j t) -> p j t", j=4)
        o1 = 4 * TB
        cvb = R[:, o1:o1 + 4 + 2 * TB].bitcast(BF16).rearrange("p (j t) -> p j t", j=4)
        o2 = o1 + 4 + 2 * TB
        u0b = R[:, o2:o2 + 60 + 2 * TB].bitcast(BF16).rearrange("p (j t) -> p j t", j=4)
        o3 = o2 + 60 + 2 * TB
        xn_views = [R[:, o3 + 512 * i:o3 + 512 * (i + 1)].bitcast(BF16) for i in range(4)]
        o4 = o3 + 2048
        junk = R[:, o4:o4 + 512].bitcast(BF16)
        assert o4 + 512 <= 11 * TB
        fT_Cv = R[:, 0:8 * TB].rearrange("p (s m t) -> p s m t", s=2, m=8)
        fT_C = lambda m, s: fT_Cv[:, s, m, :]
        cvh_s = sb("cvh_s", [128, 4, 2], BF16)
        u0h_s = sb("u0h_s", [128, 4, 30], BF16)
        slots = [sb("slot%d" % i, [128, SLOT_F], BF16) for i in range(RING)]

        r_xb = P.regs_n("xb", NT)
        r_hT = [[P.reg("hT%d_%d" % (k, s)) for s in range(NS)] for k in range(8)]
        r_yT = [[P.reg("yT%d_%d" % (k, s)) for s in range(NS)] for k in range(8)]
        r_u1 = [[P.reg("u1%d_%d" % (j, s)) for s in range(NS)] for j in range(4)]
        r_cvb = [[P.reg("cvb%d_%d" % (j, s)) for s in range(NS)] for j in range(4)]
        r_cvh = P.regs_n("cvh", 4)
        r_cvhA = P.regs_n("cvhA", 4)
        r_u0hA = P.regs_n("u0hA", 4)
        r_u0b = [[P.reg("u0b%d_%d" % (j, s)) for s in range(NS)] for j in range(4)]
        r_u0h = P.regs_n("u0h", 4)
        r_act = [[P.reg("act%d_%d" % (j, s)) for s in range(NS)] for j in range(NJ)]
        r_fTC = [[P.reg("fTC%d_%d" % (m, s)) for s in range(NS)] for m in range(8)]
        r_fTF = [[P.reg("fTF%d_%d" % (m, s)) for s in range(NS)] for m in range(8)]
        r_slot = P.regs_n("slot", RING)

        def mkpool(name, n, shape, dt):
            return Pool([(sb("%s%d" % (name, i), shape, dt), P.reg("%s%d" % (name, i))) for i in range(n)])

        tf = mkpool("tf", 10, [128, 512], F32)
        tff = Pool(tf.bufs[2:10])
        tgb = mkpool("tgb", 2, [128, 512], F32)
        tb = mkpool("tb", 4, [128, 512], BF16)
        tb2 = mkpool("tb2", 4, [128, 512], BF16)
        txp = mkpool("tx", 1, [128, 512], F32)
        txn = Pool([(xn_views[i], P.reg("txn%d" % i)) for i in range(4)])
        r_junk = P.reg("junk")

        mmr = Pool([(ps("mm%d" % i), P.reg("mm%d" % i, True)) for i in range(4)])
        tpr = Pool([(ps("tp%d" % i), P.reg("tp%d" % i, True)) for i in range(2)])
        stb = [(ps("st%d" % i), P.reg("st%d" % i, True)) for i in range(2)]
        strr = Pool(stb)

        ds_x = [P.dsem(sem("dx%d" % i), "dx%d" % i) for i in range(NT)]
        ds_o = [P.dsem(sem("do%d" % i), "do%d" % i) for i in range(NT)]
        ds_slot = [P.dsem(sem("dw%d" % i), "dw%d" % i) for i in range(RING)]
        ds_v = P.dsem(sem("dv"), "dv")
        ds_dbg = P.dsem(sem("ddbg"), "ddbg")

        def flat(ll):
            return [r for l in ll for r in l]

        r_txn = [b[1] for b in txn.bufs]

        def vcol(row, n=1):
            return vt[:, row:row + n]

        def dump(name, ap, shape, dt, regs):
            if not debug:
                return
            t = nc.dram_tensor("dbg_" + name, list(shape), dt, kind="ExternalOutput").ap()
            dbg_d[name] = t
            P.op("sp", lambda e: e.dma_start(out=t, in_=ap), reads=regs, dsem=ds_dbg)

        sched = []
        for p in range(4):
            sched.append((wada_d[p], 4096))
        for T in range(NB):
            for j in range(4):
                sched.append((winc_d[j], 2048))
                if T == 0:
                    sched.append((wada_d[4 + j], 4096))
            for j in range(4):
                sched.append((wins_d[j], 3072))
                if T == 0:
                    sched.append((wada_d[8 + j], 4096))
            for h in range(2):
                sched.append((wout_d[h], 4096))
            for g in range(11):
                sched.append((wgu_d[g], 4096))
            for rpt in range(2 if T == NB - 1 else 1):
                for m in range(8):
                    sched.append((wdn_d[m], 2816))
        state = {"issued": 0, "next": 0, "released": 0, "hold": False}

        def prefetch():
            while state["issued"] < len(sched) and state["issued"] - RING < state["released"]:
                i = state["issued"]
                k = i % RING
                src, F = sched[i]
                P.op("pool", lambda e, k=k, src=src, F=F: e.dma_start(out=slots[k][:, 0:F], in_=src),
                     writes=[r_slot[k]], dsem=ds_slot[k], name="wload%d" % i)
                state["issued"] += 1

        def take(hold=False):
            i = state["next"]
            state["next"] += 1
            if hold:
                state["hold"] = True
            elif not state["hold"]:
                state["released"] = i
            prefetch()
            assert state["issued"] > i, (state, i)
            k = i % RING
            return slots[k], r_slot[k]

        def end_hold():
            state["hold"] = False
            state["released"] = state["next"]
            prefetch()

        P.op("sp", lambda e: e.dma_start(out=v_sb[:], in_=v_d.rearrange("(t r) p -> r t p", t=2)),
             writes=[r_vsb], dsem=ds_v)
        prefetch()

        r_idf = P.reg("ident_f")
        r_blk = P.reg("blk")
        P.op("pool", lambda e: e.memset(ident_f[:], 0.0), writes=[r_idf])
        P.op("pool", lambda e: e.affine_select(out=ident_f[:], in_=ident_f[:], compare_op=ALU.not_equal, fill=1.0,
                                               base=0, pattern=[[-1, 128]], channel_multiplier=1),
             reads=[r_idf], writes=[r_idf])
        P.op("pool", lambda e: e.memset(ones_ln[:], 1.0 / 512.0), writes=[P.reg("x1")])
        P.op("pool", lambda e: e.memset(ones_d[:], 1.0 / 1024.0), writes=[P.reg("x2")])
        P.op("pool", lambda e: e.memset(onesb[:], 1.0), writes=[P.reg("x5")])
        P.op("pool", lambda e: e.memset(blk[:], 0.0), writes=[r_blk])
        P.op("pool", lambda e: e.memset(blk[0:64, 0:64], 1.0 / 64.0), writes=[r_blk])
        P.op("pool", lambda e: e.memset(blk[64:128, 64:128], 1.0 / 64.0), writes=[r_blk])
        P.op("pool", lambda e: e.memset(eps_t[:], EPS), writes=[P.reg("x3")])
        P.op("pool", lambda e: e.memset(one_t[:], 1.0), writes=[P.reg("x4")])
        P.op("pool", lambda e: e.memset(cvh_s[:], 0.0), writes=r_cvh)
        P.op("pool", lambda e: e.memset(u0h_s[:], 0.0), writes=r_u0h)
        P.op("pool", lambda e: e.memset(ctmp[:], 0.0), reads=[r_idf, r_blk], writes=[r_const, r_ctmp])
        P.op("dve", lambda e: e.tensor_copy(out=ident_b[:], in_=ident_f[:]), reads=[r_const], writes=[r_const])

        st_t, st_r = strr.next()
        for t in range(2):
            P.op("pe", lambda e, t=t: e.transpose(out=st_t[:, t * 128:(t + 1) * 128], in_=v_sb[:, t, :], identity=ident_f[:]),
                 reads=[r_vsb, r_const], writes=[st_r])
        P.op("dve", lambda e: e.tensor_copy(out=vt[:], in_=st_t[:, 0:V_ROWS]), reads=[st_r], writes=[r_vt])
        P.op("dve", lambda e: e.tensor_scalar(out=nvt[:], in0=vt[:, V_GLN:V_GLN + 8], scalar1=-1.0, scalar2=None, op0=ALU.mult),
             reads=[r_vt], writes=[r_vt])
        P.op("act", lambda e: e.activation(out=ctmp[:], in_=vt[:, V_C:V_C + 8], func=AF.Exp, scale=-1.0), reads=[r_vt], writes=[r_ctmp])
        P.op("act", lambda e: e.activation(out=ctmp[:], in_=ctmp[:], func=AF.Ln, bias=one_t[:], scale=1.0), reads=[r_ctmp, r_const], writes=[r_ctmp])
        P.op("act", lambda e: e.activation(out=ctmp[:], in_=ctmp[:], func=AF.Exp, scale=-1.0), reads=[r_ctmp], writes=[r_ctmp])
        P.op("dve", lambda e: e.tensor_tensor(out=ctmp[:], in0=ctmp[:], in1=vt[:, V_C:V_C + 8], op=ALU.mult),
             reads=[r_ctmp, r_vt], writes=[r_ctmp])
        for kc in range(8):
            P.op("dve", lambda e, kc=kc: e.tensor_scalar(out=cact_m[:, kc, :], in0=onesb[:], scalar1=ctmp[:, kc:kc + 1],
                                                        scalar2=None, op0=ALU.mult),
                 reads=[r_ctmp, r_const], writes=[r_cact])
        for k in range(31):
            P.op("dve", lambda e, k=k: e.tensor_scalar(
                out=dg[:, 0, k, :], in0=ident_b[:], scalar1=vcol(V_WCF + k * 4), scalar2=None, op0=ALU.mult),
                reads=[r_const, r_vt], writes=[r_dg[0]])
        for j in range(4):
            for k in range(3):
                P.op("dve", lambda e, j=j, k=k: e.tensor_scalar(
                    out=dg3[:, j, k, :], in0=ident_b[:], scalar1=vcol(V_WSH + k * 4 + j), scalar2=None, op0=ALU.mult),
                    reads=[r_const, r_vt], writes=[r_dg3])

        def ada_piece(p):
            slot, rsl = take()
            w = slot[:, 0:4096].rearrange("p (k c) -> p k c", k=8)
            mt, mr = mmr.next()
            for kc in range(8):
                P.op("pe", lambda e, kc=kc: e.matmul(mt[:], lhsT=cact_m[:, kc, :], rhs=w[:, kc, :],
                                                    start=(kc == 0), stop=(kc == 7)),
                     reads=[r_cact, rsl], writes=[mr])
            for q in range(4):
                P.op("dve", lambda e, q=q: e.scalar_tensor_tensor(
                    out=adscr[:], in0=mt[:, q * 128:(q + 1) * 128], scalar=1.0, in1=ident_f[:],
                    op0=ALU.mult, op1=ALU.mult, accum_out=adsum[:, q:q + 1]),
                    reads=[mr, r_const], writes=[r_adscr])
            sec = p // 2
            P.op("dve", lambda e: e.tensor_tensor(out=modT[:, 4 * p:4 * p + 4], in0=adsum[:, 0:4],
                                                 in1=vt[:, V_BADA + 4 * p:V_BADA + 4 * p + 4], op=ALU.add),
                 reads=[r_adscr, r_vt], writes=[r_mod[sec]])

        def ada_derive_gs(l):
            sc = 1 + 3 * l
            grow = V_GPRE1 if l == 0 else V_GPRE2
            P.op("dve", lambda e: e.scalar_tensor_tensor(out=gs[:, 8 * l:8 * l + 8], in0=modT[:, 8 * sc:8 * sc + 8], scalar=1.0,
                                                        in1=vt[:, grow:grow + 8], op0=ALU.add, op1=ALU.mult),
                 reads=[r_mod[sc], r_vt], writes=[r_gs[l]])

        def ada_derive_gtg(l):
            gt = 2 + 3 * l
            grow = V_GPOST1 if l == 0 else V_GPOST2
            P.op("dve", lambda e: e.tensor_tensor(out=gtg[:, 8 * l:8 * l + 8], in0=modT[:, 8 * gt:8 * gt + 8],
                                                 in1=vt[:, grow:grow + 8], op=ALU.mult),
                 reads=[r_mod[gt], r_vt], writes=[r_gtg[l]])

        def rsqrt_ops(src_ap, src_regs, dst_ap, dst_regs, scale=1.0):
            P.op("act", lambda e: e.activation(out=dst_ap, in_=src_ap, func=AF.Ln, bias=eps_t[:], scale=scale),
                 reads=list(src_regs) + [r_const], writes=list(dst_regs))
            P.op("act", lambda e: e.activation(out=dst_ap, in_=dst_ap, func=AF.Exp, scale=-0.5),
                 reads=list(dst_regs), writes=list(dst_regs))

        def prenorm_steps(T, l, s):
            sh = 0 if l == 0 else 3
            xns = []

            def st_stats():
                for ii in range(4):
                    i = s * 4 + ii
                    P.op("act", lambda e, i=i: e.activation(out=junk[:], in_=xb[:, i, :], func=AF.Square,
                                                           accum_out=ssq[:, i:i + 1]),
                         reads=[r_xb[i]], writes=[r_junk, r_ssq[s]])
                rsqrt_ops(ssq[:, 4 * s:4 * s + 4], [r_ssq[s]], rstd[:, 4 * s:4 * s + 4], [r_rstd[s]], scale=1.0 / D)

            def st_xn():
                for ii in range(4):
                    i = s * 4 + ii
                    xn, xnr = txn.next()
                    xns.append((xn, xnr))
                    P.op("dve", lambda e, i=i, xn=xn: e.tensor_scalar(out=xn[:], in0=xb[:, i, :], scalar1=rstd[:, i:i + 1],
                                                                   scalar2=None, op0=ALU.mult),
                         reads=[r_xb[i], r_rstd[s]], writes=[xnr])

            def st_half(half):
                tps = [tpr.next(), tpr.next()]
                tviews = [t[0].bitcast(BF16) for t in tps]
                for ii in range(4):
                    xn, xnr = xns[ii]
                    for kq in range(4):
                        kc = half * 4 + kq
                        tv = tviews[kq // 2]
                        c0 = (kq % 2) * 512 + ii * 128
                        P.op("pe", lambda e, kc=kc, tv=tv, c0=c0, xn=xn: e.transpose(
                            out=tv[:, c0:c0 + 128], in_=xn[:, kc * 128:(kc + 1) * 128], identity=ident_b[:]),
                            reads=[xnr, r_const], writes=[tps[kq // 2][1]])
                for kq in range(4):
                    kc = half * 4 + kq
                    tv = tviews[kq // 2]
                    c0 = (kq % 2) * 512
                    if kq // 2 == 0:
                        P.op("act", lambda e, kc=kc, tv=tv, c0=c0: e.activation(
                            out=hT[:, kc, s * 512:(s + 1) * 512], in_=tv[:, c0:c0 + 512], func=AF.Identity,
                            bias=modT[:, 8 * sh + kc:8 * sh + kc + 1], scale=gs[:, 8 * l + kc:8 * l + kc + 1]),
                            reads=[tps[kq // 2][1], r_mod[sh], r_gs[l]], writes=[r_hT[kc][s]])
                    else:
                        P.op("dve", lambda e, kc=kc, tv=tv, c0=c0: e.tensor_scalar(
                            out=hT[:, kc, s * 512:(s + 1) * 512], in0=tv[:, c0:c0 + 512],
                            scalar1=gs[:, 8 * l + kc:8 * l + kc + 1], scalar2=modT[:, 8 * sh + kc:8 * sh + kc + 1],
                            op0=ALU.mult, op1=ALU.add),
                            reads=[tps[kq // 2][1], r_mod[sh], r_gs[l]], writes=[r_hT[kc][s]])

            return [st_stats, st_xn, lambda: st_half(0), lambda: st_half(1)]

        def prenorm_part(T, l, s):
            for th in prenorm_steps(T, l, s):
                th()

        def _old_prenorm_part(T, l, s):
            sh = 0 if l == 0 else 3
            for ii in range(4):
                i = s * 4 + ii
                P.op("act", lambda e, i=i: e.activation(out=junk[:], in_=xb[:, i, :], func=AF.Square,
                                                       accum_out=ssq[:, i:i + 1]),
                     reads=[r_xb[i]], writes=[r_junk, r_ssq[s]])
            rsqrt_ops(ssq[:, 4 * s:4 * s + 4], [r_ssq[s]], rstd[:, 4 * s:4 * s + 4], [r_rstd[s]], scale=1.0 / D)
            xns = []
            for ii in range(4):
                i = s * 4 + ii
                xn, xnr = txn.next()
                xns.append((xn, xnr))
                P.op("dve", lambda e, i=i, xn=xn: e.tensor_scalar(out=xn[:], in0=xb[:, i, :], scalar1=rstd[:, i:i + 1],
                                                               scalar2=None, op0=ALU.mult),
                     reads=[r_xb[i], r_rstd[s]], writes=[xnr])
            for half in range(2):
                tps = [tpr.next(), tpr.next()]
                tviews = [t[0].bitcast(BF16) for t in tps]
                for ii in range(4):
                    xn, xnr = xns[ii]
                    for kq in range(4):
                        kc = half * 4 + kq
                        tv = tviews[kq // 2]
                        c0 = (kq % 2) * 512 + ii * 128
                        P.op("pe", lambda e, kc=kc, tv=tv, c0=c0, xn=xn: e.transpose(
                            out=tv[:, c0:c0 + 128], in_=xn[:, kc * 128:(kc + 1) * 128], identity=ident_b[:]),
                            reads=[xnr, r_const], writes=[tps[kq // 2][1]])
                for kq in range(4):
                    kc = half * 4 + kq
                    tv = tviews[kq // 2]
                    c0 = (kq % 2) * 512
                    if kq // 2 == 0:
                        P.op("act", lambda e, kc=kc, tv=tv, c0=c0, s=s: e.activation(
                            out=hT[:, kc, s * 512:(s + 1) * 512], in_=tv[:, c0:c0 + 512], func=AF.Identity,
                            bias=modT[:, 8 * sh + kc:8 * sh + kc + 1], scale=gs[:, 8 * l + kc:8 * l + kc + 1]),
                            reads=[tps[kq // 2][1], r_mod[sh], r_gs[l]], writes=[r_hT[kc][s]])
                    else:
                        P.op("dve", lambda e, kc=kc, tv=tv, c0=c0, s=s: e.tensor_scalar(
                            out=hT[:, kc, s * 512:(s + 1) * 512], in0=tv[:, c0:c0 + 512],
                            scalar1=gs[:, 8 * l + kc:8 * l + kc + 1], scalar2=modT[:, 8 * sh + kc:8 * sh + kc + 1],
                            op0=ALU.mult, op1=ALU.add),
                            reads=[tps[kq // 2][1], r_mod[sh], r_gs[l]], writes=[r_hT[kc][s]])

        def prenorm(T, l):
            for s in range(NS):
                prenorm_part(T, l, s)

        def mm_group(out_t, out_r, lhs_fn, rhs_fn, nk, reads):
            for k in range(nk):
                P.op("pe", lambda e, k=k: e.matmul(out_t[:], lhsT=lhs_fn(k), rhs=rhs_fn(k), start=(k == 0), stop=(k == nk - 1)),
                     reads=reads, writes=[out_r])

        def head_norm(yf, yfr, c, s, ring=None, fpool=None):
            sq, sqr = tb.next()
            P.op("act", lambda e: e.activation(out=sq[:], in_=yf[:], func=AF.Square), reads=[yfr], writes=[sqr])
            st_t, st_r = (ring or strr).next()
            P.op("pe", lambda e: e.matmul(st_t[:], lhsT=blk[:], rhs=sq[:], start=True, stop=True),
                 reads=[sqr, r_const], writes=[st_r])
            rr, rrr = (fpool or tf).next()
            rsqrt_ops(st_t[:], [st_r], rr[:], [rrr])
            P.op("dve", lambda e: e.scalar_tensor_tensor(out=yT[:, c, s * 512:(s + 1) * 512], in0=yf[:],
                                                        scalar=vcol(V_BETA + c), in1=rr[:], op0=ALU.mult, op1=ALU.mult),
                 reads=[yfr, rrr, r_vt], writes=[r_yT[c][s]])

        def mixer(T):
            for j in range(4):
                P.op("dve", lambda e, j=j: e.tensor_copy(out=cvb[:, j, 0:2], in_=cvh_s[:, j, :]), reads=[r_cvh[j]], writes=[r_cvhA[j]])
                P.op("dve", lambda e, j=j: e.tensor_copy(out=u0b[:, j, 0:30], in_=u0h_s[:, j, :]), reads=[r_u0h[j]], writes=[r_u0hA[j]])
            st8 = {}

            def proj(w, rsl, c, s):
                mt, mr = mmr.next()
                mm_group(mt, mr, lambda k: w[:, c, k, :], lambda k: hT[:, k, s * 512:(s + 1) * 512], 8,
                         [rsl] + [r_hT[k][s] for k in range(8)])
                return mt, mr

            def front_c(j, s):
                if s == 0:
                    slot, rsl = take()
                    st8["w"] = slot[:, 0:2048].rearrange("p (c k m) -> p c k m", c=2, k=8)
                    st8["rsl"] = rsl
                    db = (T * 4 + j) % 2
                    st8["db"] = db
                    for k in range(31):
                        if T == 0 and j == 0:
                            break
                        P.op("dve", lambda e, k=k: e.tensor_scalar(
                            out=dg[:, db, k, :], in0=ident_b[:], scalar1=vcol(V_WCF + k * 4 + j), scalar2=None, op0=ALU.mult),
                            reads=[r_const, r_vt], writes=[r_dg[db]])
                w, rsl = st8["w"], st8["rsl"]
                c0 = s * 512
                gt_, gr = proj(w, rsl, 0, s)
                at, ar = proj(w, rsl, 1, s)
                sg, sgr = tf.next()
                P.op("act", lambda e: e.activation(out=sg[:], in_=gt_[:], func=AF.Exp, scale=-1.0), reads=[gr], writes=[sgr])
                P.op("act", lambda e: e.activation(out=sg[:], in_=sg[:], func=AF.Ln, bias=one_t[:], scale=1.0), reads=[sgr, r_const], writes=[sgr])
                P.op("act", lambda e: e.activation(out=sg[:], in_=sg[:], func=AF.Exp, scale=-1.0), reads=[sgr], writes=[sgr])
                P.op("dve", lambda e: e.tensor_tensor(out=u0b[:, j, 30 + c0:30 + c0 + 512], in0=sg[:], in1=at[:], op=ALU.mult),
                     reads=[sgr, ar], writes=[r_u0b[j][s]])
                if T == 0 and s == NS - 1:
                    ada_piece(4 + j)
                return (st8["db"],)

            def back_c(j, s, db):
                c0 = s * 512
                prev_u0 = r_u0b[j][s - 1] if s > 0 else r_u0hA[j]
                ct2, cr2 = mmr.next()
                mm_group(ct2, cr2, lambda k: dg[:, db, k, :], lambda k: u0b[:, j, c0 + k:c0 + k + 512], 31,
                         [r_dg[db], r_u0b[j][s], prev_u0])
                P.op("act", lambda e: e.activation(out=u1[:, j, c0:c0 + 512], in_=ct2[:], func=AF.Identity,
                                                   bias=vcol(V_BCF + j), scale=1.0),
                     reads=[cr2, r_vt], writes=[r_u1[j][s]])
                if s == NS - 1 and T + 1 < NB:
                    P.op("dve", lambda e: e.tensor_copy(out=u0h_s[:, j, :], in_=u0b[:, j, TB:TB + 30]),
                         reads=[r_u0b[j][NS - 1]], writes=[r_u0h[j]])

            pend = None
            for j in range(4):
                for s in range(NS):
                    cur = (j, s) + front_c(j, s)
                    if pend is not None:
                        back_c(*pend)
                    pend = cur
            back_c(*pend)

            tfs = Pool(tf.bufs[8:10])
            def front_s(j, s):
                if s == 0:
                    slot, rsl = take()
                    st8["w"] = slot[:, 0:3072].rearrange("p (c k m) -> p c k m", c=3, k=8)
                    st8["rsl"] = rsl
                w, rsl = st8["w"], st8["rsl"]
                c0 = s * 512
                gct, gcr = proj(w, rsl, 0, s)
                vtt, vr = proj(w, rsl, 1, s)
                gcs, gcsr = tfs.next()
                P.op("act", lambda e: e.copy(out=gcs[:], in_=gct[:]), reads=[gcr], writes=[gcsr])
                P.op("dve", lambda e: e.tensor_tensor(out=cvb[:, j, 2 + c0:2 + c0 + 512], in0=gcs[:], in1=vtt[:], op=ALU.mult),
                     reads=[gcsr, vr], writes=[r_cvb[j][s]])
                gbt, gbr = proj(w, rsl, 2, s)
                gbs, gbsr = tgb.next()
                P.op("act", lambda e: e.copy(out=gbs[:], in_=gbt[:]), reads=[gbr], writes=[gbsr])
                if T == 0 and s == NS - 1:
                    ada_piece(8 + j)
                return (gbs, gbsr)

            ysb = [tf.bufs[8], tf.bufs[9]]
            sqb = [tb2.bufs[2], tb2.bufs[3]]

            def back_a(n, j, s, gbs, gbsr):
                c0 = s * 512
                prev_cv = r_cvb[j][s - 1] if s > 0 else r_cvhA[j]
                ct, cr = mmr.next()
                mm_group(ct, cr, lambda k: dg3[:, j, k, :], lambda k: cvb[:, j, c0 + k:c0 + k + 512], 3,
                         [r_dg3, r_cvb[j][s], prev_cv])
                ysf, ysr = ysb[n % 2]
                P.op("dve", lambda e: e.scalar_tensor_tensor(
                    out=ysf[:], in0=ct[:], scalar=vcol(V_BSH + j), in1=gbs[:], op0=ALU.add, op1=ALU.mult),
                    reads=[cr, gbsr, r_vt], writes=[ysr])
                sq, sqr = sqb[n % 2]
                P.op("act", lambda e: e.activation(out=sq[:], in_=ysf[:], func=AF.Square), reads=[ysr], writes=[sqr])
                if s == NS - 1 and T + 1 < NB:
                    P.op("dve", lambda e: e.tensor_copy(out=cvh_s[:, j, :], in_=cvb[:, j, TB:TB + 2]),
                         reads=[r_cvb[j][NS - 1]], writes=[r_cvh[j]])

            def back_b(n, j, s):
                ysf, ysr = ysb[n % 2]
                sq, sqr = sqb[n % 2]
                st_t, st_r = mmr.next()
                P.op("pe", lambda e: e.matmul(st_t[:], lhsT=blk[:], rhs=sq[:], start=True, stop=True),
                     reads=[sqr, r_const], writes=[st_r])
                rr, rrr = txp.next()
                rsqrt_ops(st_t[:], [st_r], rr[:], [rrr])
                P.op("dve", lambda e: e.scalar_tensor_tensor(out=yT[:, j, s * 512:(s + 1) * 512], in0=ysf[:],
                                                            scalar=vcol(V_BETA + j), in1=rr[:], op0=ALU.mult, op1=ALU.mult),
                     reads=[ysr, rrr, r_vt], writes=[r_yT[j][s]])

            short_steps = []
            sst = {"pa": None, "pb": None, "n": 0}

            def sstep_parts(j, s):
                box = {}
                c0 = s * 512

                def half(c, lo, hi, post=None):
                    def th():
                        if c == 0 and lo == 0 and s == 0:
                            slot, rsl = take()
                            st8["w"] = slot[:, 0:3072].rearrange("p (c k m) -> p c k m", c=3, k=8)
                            st8["rsl"] = rsl
                        w, rsl = st8["w"], st8["rsl"]
                        if lo == 0:
                            box[c] = mmr.next()
                        mt, mr = box[c]
                        for k in range(lo, hi):
                            P.op("pe", lambda e, k=k: e.matmul(mt[:], lhsT=w[:, c, k, :], rhs=hT[:, k, s * 512:(s + 1) * 512],
                                                               start=(k == 0), stop=(k == 7)),
                                 reads=[rsl] + [r_hT[kk][s] for kk in range(8)], writes=[mr])
                        if post is not None:
                            post()
                    return th

                def post_v():
                    (gct, gcr), (vtt, vr) = box[0], box[1]
                    gcs, gcsr = txp.next()
                    P.op("act", lambda e: e.copy(out=gcs[:], in_=gct[:]), reads=[gcr], writes=[gcsr])
                    P.op("dve", lambda e: e.tensor_tensor(out=cvb[:, j, 2 + c0:2 + c0 + 512], in0=gcs[:], in1=vtt[:], op=ALU.mult),
                         reads=[gcsr, vr], writes=[r_cvb[j][s]])

                def post_gb():
                    gbt, gbr = box[2]
                    gbs, gbsr = tgb.next()
                    P.op("act", lambda e: e.copy(out=gbs[:], in_=gbt[:]), reads=[gbr], writes=[gbsr])
                    if T == 0 and s == NS - 1:
                        ada_piece(8 + j)
                    if sst["pa"] is not None:
                        back_a(*sst["pa"])
                    if sst["pb"] is not None:
                        back_b(*sst["pb"])
                    sst["pb"] = sst["pa"][0:3] if sst["pa"] is not None else None
                    sst["pa"] = (sst["n"], j, s, gbs, gbsr)
                    sst["n"] += 1

                return [half(0, 0, 4), half(0, 4, 8), half(1, 0, 4), half(1, 4, 8, post_v),
                        half(2, 0, 4), half(2, 4, 8, post_gb)]

            for j in range(4):
                for s in range(NS):
                    short_steps.extend(sstep_parts(j, s))
            def sflush():
                back_a(*sst["pa"])
                if sst["pb"] is not None:
                    back_b(*sst["pb"])
                back_b(*sst["pa"][0:3])
            short_steps.append(sflush)

            def ln_rows(s):
                c0 = s * 512
                if s == 0:
                    (mean_t, mean_r), (msq_t, msq_r) = stb[0], stb[1]
                else:
                    (mean_t, mean_r), (msq_t, msq_r) = tpr.next(), tpr.next()
                ubs = [tb.next() for _ in range(4)] if s == 0 else [tb2.bufs[j % 2] for j in range(4)]
                t1s = [(u1[:, j, c0:c0 + 512], r_u1[j][s]) for j in range(4)]
                ezs = [tf.bufs[4 * s + j] for j in range(4)]
                m2, m2r = ezs[0]
                rows = []

                def per_j(fn):
                    rows.append([(lambda j=j: fn(j)) for j in range(4)])

                def cast_p(j):
                    ub, ubr = ubs[j]
                    P.op("dve", lambda e: e.tensor_copy(out=ub[:], in_=u1[:, j, c0:c0 + 512]), reads=[r_u1[j][s]], writes=[ubr])

                def cast_c(j):
                    ub, ubr = ubs[j]
                    P.op("pe", lambda e: e.matmul(mean_t[:], lhsT=ones_ln[:], rhs=ub[:], start=(j == 0), stop=(j == 3)),
                         reads=[ubr, r_const], writes=[mean_r])

                def sq_p(j):
                    us, usr = ubs[j]
                    P.op("act", lambda e: e.activation(out=us[:], in_=u1[:, j, c0:c0 + 512], func=AF.Square), reads=[r_u1[j][s]], writes=[usr])

                def sq_c(j):
                    us, usr = ubs[j]
                    P.op("pe", lambda e: e.matmul(msq_t[:], lhsT=ones_ln[:], rhs=us[:], start=(j == 0), stop=(j == 3)),
                         reads=[usr, r_const], writes=[msq_r])

                rows.append([lambda: cast_p(0)] + [(lambda j=j: (cast_p(j), cast_c(j - 1))) for j in range(1, 4)])
                rows.append([lambda: (cast_c(3), sq_p(0))] + [(lambda j=j: (sq_p(j), sq_c(j - 1))) for j in range(1, 4)])

                def r_var():
                    P.op("act", lambda e: e.activation(out=m2[:], in_=mean_t[:], func=AF.Square), reads=[mean_r], writes=[m2r])
                    P.op("dve", lambda e: e.tensor_tensor(out=m2[:], in0=msq_t[:], in1=m2[:], op=ALU.subtract), reads=[msq_r, m2r], writes=[m2r])
                    rsqrt_ops(m2[:], [m2r], msq_t[:], [msq_r])
                rows.append([lambda: (sq_c(3), r_var())])

                def sub(j):
                    t1, t1r = t1s[j]
                    P.op("dve", lambda e: e.tensor_tensor(out=t1, in0=t1, in1=mean_t[:], op=ALU.subtract), reads=[t1r, mean_r], writes=[t1r])
                per_j(sub)

                def mul(j):
                    t1, t1r = t1s[j]
                    P.op("dve", lambda e: e.tensor_tensor(out=t1, in0=t1, in1=msq_t[:], op=ALU.mult), reads=[t1r, msq_r], writes=[t1r])
                per_j(mul)

                def e1(j):
                    t1, t1r = t1s[j]
                    ez, ezr = ezs[j]
                    P.op("act", lambda e: e.activation(out=ez[:], in_=t1, func=AF.Exp, bias=nvt[:, 4 + j:5 + j], scale=nvt[:, j:j + 1]),
                         reads=[t1r, r_vt], writes=[ezr])
                per_j(e1)

                def ln1(j):
                    t1, t1r = t1s[j]
                    ez, ezr = ezs[j]
                    P.op("act", lambda e: e.activation(out=ez[:], in_=ez[:], func=AF.Ln, bias=one_t[:], scale=1.0), reads=[ezr, r_const], writes=[ezr])
                    P.op("dve", lambda e: e.tensor_scalar(out=t1, in0=t1, scalar1=vcol(V_GLN + j), scalar2=vcol(V_BLN + j),
                                                          op0=ALU.mult, op1=ALU.add),
                         reads=[t1r, r_vt], writes=[t1r])
                per_j(ln1)

                def e2(j):
                    ez, ezr = ezs[j]
                    P.op("act", lambda e: e.activation(out=ez[:], in_=ez[:], func=AF.Exp, scale=-1.0), reads=[ezr], writes=[ezr])
                per_j(e2)

                def silu(j):
                    t1, t1r = t1s[j]
                    ez, ezr = ezs[j]
                    P.op("dve", lambda e: e.tensor_tensor(out=t1, in0=t1, in1=ez[:], op=ALU.mult), reads=[t1r, ezr], writes=[t1r])
                per_j(silu)

                def hsq_p(j):
                    t1, t1r = t1s[j]
                    sq_, sqr = ubs[j]
                    P.op("act", lambda e: e.activation(out=sq_[:], in_=t1, func=AF.Square), reads=[t1r], writes=[sqr])

                def hsq_c(j):
                    sq_, sqr = ubs[j]
                    P.op("pe", lambda e: e.matmul(mean_t[:], lhsT=blk[:], rhs=sq_[:], start=True, stop=True),
                         reads=[sqr, r_const], writes=[mean_r])
                    ez, ezr = ezs[j]
                    P.op("act", lambda e: e.activation(out=ez[:], in_=mean_t[:], func=AF.Ln, bias=eps_t[:], scale=1.0),
                         reads=[mean_r, r_const], writes=[ezr])

                def hexp(j):
                    ez, ezr = ezs[j]
                    P.op("act", lambda e: e.activation(out=ez[:], in_=ez[:], func=AF.Exp, scale=-0.5), reads=[ezr], writes=[ezr])

                rows.append([lambda: hsq_p(0)] + [(lambda j=j: (hsq_p(j), hsq_c(j - 1))) for j in range(1, 4)])
                rows.append([lambda: (hsq_c(3), hexp(0))] + [(lambda j=j: hexp(j)) for j in range(1, 4)])

                def out(j):
                    t1, t1r = t1s[j]
                    ez, ezr = ezs[j]
                    P.op("dve", lambda e: e.scalar_tensor_tensor(
                        out=yT[:, 4 + j, s * 512:(s + 1) * 512], in0=t1, scalar=vcol(V_BETA + 4 + j), in1=ez[:],
                        op0=ALU.mult, op1=ALU.mult),
                        reads=[t1r, ezr, r_vt], writes=[r_yT[4 + j][s]])
                per_j(out)
                return rows

            rows0 = ln_rows(0)
            rows1 = ln_rows(1)
            LAG = 0
            ln_list = []
            for k in range(len(rows0) + LAG):
                ra = rows0[k] if k < len(rows0) else []
                rb = rows1[k - LAG] if 0 <= k - LAG < len(rows1) else []
                for i in range(max(len(ra), len(rb))):
                    if i < len(ra):
                        ln_list.append(ra[i])
                    if i < len(rb):
                        ln_list.append(rb[i])
            interleave(ln_list, short_steps + [(lambda: None)] * 8)

        def out_proj_parts(T, l, srcT, r_src, nk, fT, r_fT):
            ssq_b = [stb[s] for s in range(NS)]
            st = {"pend": None}

            def flush():
                if st["pend"] is not None:
                    st["pend"]()
                    st["pend"] = None

            def main(m, s, lhs, rsl):
                mt, mr = mmr.next()
                mm_group(mt, mr, lhs, lambda k: srcT[:, k, s * 512:(s + 1) * 512], nk,
                         [rsl] + [r_src[k][s] for k in range(nk)])
                P.op("act", lambda e: e.copy(out=fT(m, s), in_=mt[:]), reads=[mr], writes=[r_fT[m][s]])
                sq, sqr = tb.next()
                P.op("act", lambda e: e.activation(out=sq[:], in_=mt[:], func=AF.Square), reads=[mr], writes=[sqr])
                flush()
                st["pend"] = (lambda: P.op(
                    "pe", lambda e: e.matmul(ssq_b[s][0][:], lhsT=ones_d[:], rhs=sq[:], start=(m == 0), stop=(m == 7)),
                    reads=[sqr, r_const], writes=[ssq_b[s][1]]))

            def main_parts(m, s, lhs, rsl, parts=2):
                box = {}
                per = nk // parts
                ths = []
                for p in range(parts):
                    lo, hi = p * per, (nk if p == parts - 1 else (p + 1) * per)

                    def th(lo=lo, hi=hi, p=p):
                        if p == 0:
                            box["mt"], box["mr"] = mmr.next()
                        mt, mr = box["mt"], box["mr"]
                        for k in range(lo, hi):
                            P.op("pe", lambda e, k=k: e.matmul(mt[:], lhsT=lhs(k), rhs=srcT[:, k, s * 512:(s + 1) * 512],
                                                               start=(k == 0), stop=(k == nk - 1)),
                                 reads=[rsl] + [r_src[kk][s] for kk in range(nk)], writes=[mr])
                        if p == parts - 1:
                            P.op("act", lambda e: e.copy(out=fT(m, s), in_=mt[:]), reads=[mr], writes=[r_fT[m][s]])
                            sq, sqr = tb.next()
                            P.op("act", lambda e: e.activation(out=sq[:], in_=mt[:], func=AF.Square), reads=[mr], writes=[sqr])
                            flush()
                            st["pend"] = (lambda: P.op(
                                "pe", lambda e: e.matmul(ssq_b[s][0][:], lhsT=ones_d[:], rhs=sq[:], start=(m == 0), stop=(m == 7)),
                                reads=[sqr, r_const], writes=[ssq_b[s][1]]))
                    ths.append(th)
                return ths

            def tail(s):
                th = []
                th.append(lambda: rsqrt_ops(ssq_b[s][0][:], [ssq_b[s][1]], ssq_b[s][0][:], [ssq_b[s][1]]))
                tgp = [tf.bufs[0], tf.bufs[1]]
                tst = {"T": None, "A": None}

                def do_T(m):
                    tg, tgr = tgp[m % 2]
                    tp_t, tp_r = tpr.next()
                    for ii in range(4):
                        P.op("pe", lambda e, ii=ii: e.transpose(
                            out=tp_t[:, ii * 128:(ii + 1) * 128], in_=tg[:, ii * 128:(ii + 1) * 128], identity=ident_f[:]),
                            reads=[tgr, r_const], writes=[tp_r])
                    xv = xb[:, s * 4:s * 4 + 4, m * 128:(m + 1) * 128]
                    return (lambda: P.op(
                        "dve", lambda e: e.tensor_tensor(out=xv, in0=xv, in1=tp_t[:].rearrange("p (i c) -> p i c", i=4), op=ALU.add),
                        reads=[tp_r] + [r_xb[s * 4 + ii] for ii in range(4)], writes=[r_xb[s * 4 + ii] for ii in range(4)]))

                def advance():
                    newA = tst["T"]() if tst["T"] is not None else None
                    if tst["A"] is not None:
                        tst["A"]()
                    tst["A"] = newA
                    tst["T"] = None

                def step(m):
                    tg, tgr = tgp[m % 2]
                    P.op("dve", lambda e: e.scalar_tensor_tensor(
                        out=tg[:], in0=fT(m, s), scalar=gtg[:, 8 * l + m:8 * l + m + 1],
                        in1=ssq_b[s][0][:], op0=ALU.mult, op1=ALU.mult),
                        reads=[r_fT[m][s], r_gtg[l], ssq_b[s][1]], writes=[tgr])
                    advance()
                    tst["T"] = (lambda: do_T(m))

                for m in range(8):
                    th.append(lambda m=m: step(m))

                def fin():
                    advance()
                    advance()
                th.append(fin)
                return th

            return main, flush, tail, main_parts

        def ffn_group(w, rsl, g, jj, s):
            j = 2 * g + jj
            gt_, gr = mmr.next()
            mm_group(gt_, gr, lambda k: w[:, jj, 0, k, :], lambda k: hT[:, k, s * 512:(s + 1) * 512], 8,
                     [rsl] + [r_hT[k][s] for k in range(8)])
            ut, ur = mmr.next()
            mm_group(ut, ur, lambda k: w[:, jj, 1, k, :], lambda k: hT[:, k, s * 512:(s + 1) * 512], 8,
                     [rsl] + [r_hT[k][s] for k in range(8)])
            sl, slr = tff.next()
            P.op("act", lambda e: e.activation(out=sl[:], in_=gt_[:], func=AF.Silu), reads=[gr], writes=[slr])
            P.op("dve", lambda e: e.tensor_tensor(out=actT[:, j, s * 512:(s + 1) * 512], in0=sl[:], in1=ut[:], op=ALU.mult),
                 reads=[slr, ur], writes=[r_act[j][s]])

        def ffn_group_parts(w, rsl, g, jj, s):
            j = 2 * g + jj
            box = {}
            rd = [rsl] + [r_hT[k][s] for k in range(8)]

            def half(u, lo, hi, last=False):
                def th():
                    if lo == 0:
                        box[u] = mmr.next()
                    t_, r_ = box[u]
                    for k in range(lo, hi):
                        P.op("pe", lambda e, k=k: e.matmul(t_[:], lhsT=w[:, jj, u, k, :], rhs=hT[:, k, s * 512:(s + 1) * 512],
                                                           start=(k == 0), stop=(k == 7)),
                             reads=rd, writes=[r_])
                    if last:
                        (gt_, gr), (ut, ur) = box[0], box[1]
                        sl, slr = tff.next()
                        P.op("act", lambda e: e.activation(out=sl[:], in_=gt_[:], func=AF.Silu), reads=[gr], writes=[slr])
                        P.op("dve", lambda e: e.tensor_tensor(out=actT[:, j, s * 512:(s + 1) * 512], in0=sl[:], in1=ut[:], op=ALU.mult),
                             reads=[slr, ur], writes=[r_act[j][s]])
                return th
            return [half(0, 0, 4), half(0, 4, 8), half(1, 0, 4), half(1, 4, 8, last=True)]

        def interleave(a, b):
            na, nb = len(a), len(b)
            ia = ib = 0
            while ia < na or ib < nb:
                if ib >= nb or (ia < na and ia * nb <= ib * na):
                    a[ia]()
                    ia += 1
                else:
                    b[ib]()
                    ib += 1

        def load_x(T, tiles=None, after=()):
            for i in (range(NT) if tiles is None else tiles):
                r0 = (T * NT + i) * 128
                P.op("sp", lambda e, i=i, r0=r0: e.dma_start(out=xb[:, i, :], in_=x_d[r0:r0 + 128, :]),
                     reads=list(after), writes=[r_xb[i]], dsem=ds_x[i])

        def store_out(T):
            for i in range(NT):
                r0 = (T * NT + i) * 128
                P.op("sp", lambda e, i=i, r0=r0: e.dma_start(out=out_d[r0:r0 + 128, :], in_=xb[:, i, :]),
                     reads=[r_xb[i]], dsem=ds_o[i])

        try:
            load_x(0, tiles=range(0, 4))
            if STOP["at"] == "load":
                raise _StopBuild()
            for p in range(4):
                ada_piece(p)
            load_x(0, tiles=range(4, NT))
            ada_derive_gs(0)
            if STOP["at"] == "ada0":
                raise _StopBuild()
            prenorm(0, 0)
            for T in range(NB):
                if STOP["at"] == "prenorm0":
                    raise _StopBuild()
                if T == 0:
                    dump("h1T", hT[:], [128, 8, TB], BF16, [r for k in r_hT for r in k])
                mixer(T)
                if STOP["at"] == "mixer":
                    raise _StopBuild()
                if T == 0:
                    dump("yT", yT[:], [128, 8, TB], BF16, [r for k in r_yT for r in k])
                    ada_derive_gtg(0)
                    ada_derive_gs(1)
                    ada_derive_gtg(1)
                P.fence(flat(r_u1) + flat(r_cvb) + flat(r_u0b) + r_cvhA + r_u0hA, flat(r_fTC))
                mainC, flushC, tailC, mainC_parts = out_proj_parts(T, 0, yT, r_yT, 8, fT_C, r_fTC)
                pcs = [take(hold=True), take(hold=True)]

                def lhsC(m, pcs=pcs):
                    slot, rsl = pcs[m // 4]
                    w = slot[:, 0:4096].rearrange("p (m k c) -> p m k c", m=4, k=8)
                    return (lambda k: w[:, m % 4, k, :]), rsl

                for m in range(8):
                    mainC(m, 0, *lhsC(m))
                flushC()
                A = tailC(0) + prenorm_steps(T, 1, 0)
                B = [t for m in range(8) for t in mainC_parts(m, 1, *lhsC(m))] + [flushC]
                interleave(A, B)
                end_hold()
                NE = 3
                P.fence([r_fTC[m][0] for m in range(8)], [r_act[j][ss] for j in range(8) for ss in range(NS)])
                fp = [take(hold=True) for _ in range(NE)]
                fw = [(sl[:, 0:4096].rearrange("p (j u k m) -> p j u k m", j=2, u=2, k=8), rs) for sl, rs in fp]
                A2 = tailC(1) + prenorm_steps(T, 1, 1)
                B2 = [t for g in range(NE) for jj in range(2) for t in ffn_group_parts(fw[g][0], fw[g][1], g, jj, 0)]
                interleave(A2, B2)
                if T == 0:
                    dump("x1", xb[:], [128, NT, D], F32, r_xb)
                P.fence(flat(r_fTC) + r_txn + [r_junk], [r_act[j][ss] for j in range(8, NJ) for ss in range(NS)])
                for g in range(NE):
                    for jj in range(2):
                        ffn_group(fw[g][0], fw[g][1], g, jj, 1)
                end_hold()
                for g in range(NE, 11):
                    slot, rsl = take()
                    w = slot[:, 0:4096].rearrange("p (j u k m) -> p j u k m", j=2, u=2, k=8)
                    for jj in range(2):
                        for ss in range(NS):
                            ffn_group(w, rsl, g, jj, ss)
                if T == 0:
                    dump("actT", actT[:], [128, NJ, TB], BF16, [r for k in r_act for r in k])
                P.fence(flat(r_hT) + flat(r_yT), flat(r_fTF))

                def after_f(s, T=T):
                    for ii in range(4):
                        i = 4 * s + ii
                        r0 = (T * NT + i) * 128
                        P.op("sp", lambda e, i=i, r0=r0: e.dma_start(out=out_d[r0:r0 + 128, :], in_=xb[:, i, :]),
                             reads=[r_xb[i]], dsem=ds_o[i])
                    if T + 1 < NB:
                        for ii in range(4):
                            i = 4 * s + ii
                            r0 = ((T + 1) * NT + i) * 128
                            P.op("sp", lambda e, i=i, r0=r0: e.dma_start(out=xb[:, i, :], in_=x_d[r0:r0 + 128, :]),
                                 writes=[r_xb[i]], dsem=ds_x[i])

                mainF, flushF, tailF, mainF_parts = out_proj_parts(T, 1, actT, r_act, NJ, fT_F, r_fTF)
                if T + 1 < NB:
                    for m in range(8):
                        slot, rsl = take()
                        w = slot[:, 0:2816].rearrange("p (j c) -> p j c", j=NJ)
                        for ss in range(NS):
                            mainF(m, ss, (lambda k, w=w: w[:, k, :]), rsl)
                    flushF()
                    for ss in range(NS):
                        for th in tailF(ss):
                            th()
                        after_f(ss)
                else:
                    def mF(m, ss):
                        slot, rsl = take()
                        w = slot[:, 0:2816].rearrange("p (j c) -> p j c", j=NJ)
                        mainF(m, ss, (lambda k, w=w: w[:, k, :]), rsl)

                    for m in range(8):
                        mF(m, 0)
                    flushF()
                    def mF_parts(m, ss):
                        box = {}

                        def first():
                            slot, rsl = take()
                            box["w"] = slot[:, 0:2816].rearrange("p (j c) -> p j c", j=NJ)
                            box["ths"] = mainF_parts(m, ss, (lambda k: box["w"][:, k, :]), rsl, parts=4)
                            box["ths"][0]()
                        return [first] + [(lambda i=i: box["ths"][i]()) for i in range(1, 4)]

                    interleave([t for m in range(8) for t in mF_parts(m, 1)] + [flushF],
                               tailF(0) + [lambda: after_f(0)])
                    for th in tailF(1):
                        th()
                    after_f(1)
                if T + 1 < NB:
                    P.fence(flat(r_act), r_txn + [r_junk] + flat(r_u1) + flat(r_cvb) + flat(r_u0b) + r_cvhA + r_u0hA)
                    P.fence(flat(r_fTF), flat(r_hT) + flat(r_yT))
                    for ss in range(NS):
                        prenorm_part(T + 1, 0, ss)

        except _StopBuild:
            store_out(0)
        if DEBUG.get("sbuf"):
            print("SBUF bytes remaining per partition:", nc.sbuf_bytes_remaining)
        P.emit(esems)
    return nc, dbg_d


def _layouts(inp):
    f = lambda a: np.ascontiguousarray(np.asarray(a, dtype=np.float32))
    w_ada = f(inp["w_ada"])[0]
    w_in = f(inp["w_in"])[0]
    w_out = f(inp["w_out"])[0]
    w_gu = f(inp["w_gate_up"])[0]
    w_dn = f(inp["w_down"])[0]
    wada = f(w_ada.reshape(8, 128, 12, 512).transpose(2, 1, 0, 3)).reshape(12, 128, 4096)
    t = w_in.reshape(8, 128, 5, 4, 128)[:, :, [1, 2, 0, 3, 4], :, :]
    tt = f(t.transpose(3, 1, 2, 0, 4))
    winc = f(tt[:, :, [4, 3]]).reshape(4, 128, 2048)
    wins = f(tt[:, :, 0:3]).reshape(4, 128, 3072)
    t = w_out.reshape(8, 128, 2, 4, 128)
    wout = f(t.transpose(2, 1, 3, 0, 4)).reshape(2, 128, 4096)
    t = w_gu.reshape(8, 128, 2, 11, 2, 128)
    wgu = f(t.transpose(3, 1, 4, 2, 0, 5)).reshape(11, 128, 4096)
    t = w_dn.reshape(NJ, 128, 8, 128)
    wdn = f(t.transpose(2, 1, 0, 3)).reshape(8, 128, 2816)
    return wada, winc, wins, wout, wgu, wdn


def _vmat(inp, b):
    rows = np.zeros((V_ROWS, 128), np.float32)

    def put(r0, a):
        a = np.asarray(a, dtype=np.float32).reshape(-1, 128)
        rows[r0:r0 + a.shape[0]] = a

    put(V_C, inp["c"][b])
    put(V_BADA, inp["b_ada"][0])
    put(V_GPRE1, inp["g_pre_mix"][0])
    put(V_GPOST1, inp["g_post_mix"][0])
    put(V_GPRE2, inp["g_pre_ffn"][0])
    put(V_GPOST2, inp["g_post_ffn"][0])
    put(V_BETA, inp["beta_mix"][0])
    put(V_WSH, inp["w_short"][0])
    put(V_BSH, inp["b_short"][0])
    put(V_WCF, inp["w_cfm_dw"][0])
    put(V_BCF, inp["b_cfm_dw"][0])
    put(V_GLN, inp["g_cfm_ln"][0])
    put(V_BLN, inp["b_cfm_ln"][0])
    return rows


_CACHE = {}


def kernel(**inputs):
    inp = {k: np.asarray(v) for k, v in inputs.items()}
    debug = bool(DEBUG.get("on"))
    key = ("nc", debug)
    if key not in _CACHE:
        _CACHE[key] = build_program(debug=debug)
    nc, dbg_d = _CACHE[key]
    wada, winc, wins, wout, wgu, wdn = _layouts(inp)
    x = np.asarray(inp["x"], dtype=np.float32)
    in_maps = []
    for b in range(NCORE):
        in_maps.append({
            "x": np.ascontiguousarray(x[b]),
            "vmat": _vmat(inp, b),
            "wada": wada, "winc": winc, "wins": wins, "wout": wout, "wgu": wgu, "wdn": wdn,
        })
    res = run_bass_kernel_spmd(nc, in_maps, core_ids=list(range(NCORE)))
    if debug:
        DEBUG["results"] = res.results
    out = np.stack([np.asarray(res.results[b]["out"], dtype=np.float32) for b in range(NCORE)], axis=0)
    return out
```

```python
import numpy as np
from contextlib import ExitStack
import concourse.bass as bass
import concourse.mybir as mybir
from concourse.bass_utils import run_bass_kernel_spmd

F32 = mybir.dt.float32
BF16 = mybir.dt.bfloat16
ALU = mybir.AluOpType
AF = mybir.ActivationFunctionType

D = 1024
S = 2048
NCORE = 8
NS = 2
TB = NS * 512
NB = S // TB
NT = TB // 128
HID = 2816
NJ = HID // 128
EPS = 1e-6
RING = 4
SLOT_F = 5120

V_C = 0
V_BADA = 8
V_GPRE1 = 56
V_GPOST1 = 64
V_GPRE2 = 72
V_GPOST2 = 80
V_BETA = 88
V_WSH = 96
V_BSH = 108
V_WCF = 112
V_BCF = 236
V_GLN = 240
V_BLN = 244
V_ROWS = 256

DEBUG = {}
STOP = {"at": None}


class _StopBuild(Exception):
    pass


class Reg:
    __slots__ = ("name", "w", "rs", "fence", "psum")

    def __init__(self, name, psum=False):
        self.name = name
        self.w = None
        self.rs = []
        self.fence = ()
        self.psum = psum


class DmaSem:
    __slots__ = ("sem", "count", "name")

    def __init__(self, sem, name):
        self.sem = sem
        self.count = 0
        self.name = name


class Op:
    __slots__ = ("eng", "calls", "deps", "sig", "val", "dsem", "dval", "name", "idx")


class _Rec:
    def __init__(self):
        self.calls = []

    def __getattr__(self, name):
        def f(*a, **kw):
            self.calls.append((name, a, kw))
            return self
        return f


class Prog:
    ENGS = ("pe", "act", "dve", "pool", "sp")

    def __init__(self, nc):
        self.nc = nc
        self.q = {e: [] for e in self.ENGS}
        self.n = 0
        self.dsems = []

    def reg(self, name, psum=False):
        return Reg(name, psum)

    def regs_n(self, name, n):
        return [Reg("%s%d" % (name, i)) for i in range(n)]

    def dsem(self, sem, name):
        d = DmaSem(sem, name)
        self.dsems.append(d)
        return d

    def fence(self, regs_old, regs_new):
        ops = []
        for r in regs_old:
            if r.w is not None:
                ops.append(r.w)
            ops.extend(r.rs)
            ops.extend(r.fence)
        last = {}
        keep = []
        for o in ops:
            if o.dsem is not None:
                keep.append(o)
            elif o.eng not in last or o.idx > last[o.eng].idx:
                last[o.eng] = o
        keep.extend(last.values())
        for r in regs_new:
            r.fence = tuple(keep)

    def op(self, eng, fn, reads=(), writes=(), dsem=None, name=""):
        o = Op()
        o.eng = eng
        rec = _Rec()
        fn(rec)
        o.calls = rec.calls
        o.sig = False
        o.val = None
        o.dsem = dsem
        o.name = name
        o.idx = self.n
        self.n += 1
        deps = {}
        for r in reads:
            if r.w is not None:
                deps[id(r.w)] = r.w
            for f in r.fence:
                deps[id(f)] = f
            if r.psum:
                for x in r.rs:
                    if x.eng != eng:
                        deps[id(x)] = x
        for r in writes:
            if r.w is not None:
                deps[id(r.w)] = r.w
            for x in r.rs:
                deps[id(x)] = x
            for f in r.fence:
                deps[id(f)] = f
        dl = []
        for d in deps.values():
            if d.dsem is not None:
                dl.append((d, d.dsem.count))
            else:
                if d.eng == "pe" and eng == "pe" and dsem is None:
                    continue
                d.sig = True
                dl.append((d, None))
        o.deps = dl
        if dsem is not None:
            dsem.count += 16
            o.dval = dsem.count
        else:
            o.dval = None
        for r in reads:
            r.rs.append(o)
        for r in writes:
            r.w = o
            r.rs = []
        self.q[eng].append(o)
        return o

    def emit(self, esems, final_eng="sp"):
        nc = self.nc
        for e in self.ENGS:
            c = 0
            for o in self.q[e]:
                if o.dsem is None and o.sig:
                    c += 1
                    o.val = c
        prog = self

        def run(e, eng):
            waited = {}
            for o in prog.q[e]:
                for d, dv in o.deps:
                    if d.dsem is not None:
                        sem, val = d.dsem.sem, dv
                    else:
                        sem, val = esems[d.eng], d.val
                    k = id(sem)
                    if waited.get(k, 0) < val:
                        eng.wait_ge(sem, val)
                        waited[k] = val
                ins = None
                for (mname, a, kw) in o.calls:
                    ins = getattr(eng, mname)(*a, **kw)
                if o.dsem is not None:
                    ins.then_inc(o.dsem.sem, 16)
                elif o.sig:
                    ins.then_inc(esems[e], 1)
            if e == final_eng:
                for d in prog.dsems:
                    if d.count > 0:
                        eng.wait_ge(d.sem, d.count)

        with nc.Block() as block:

            @block.tensor
            def _(eng):
                run("pe", eng)

            @block.scalar
            def _(eng):
                run("act", eng)

            @block.vector
            def _(eng):
                run("dve", eng)

            @block.gpsimd
            def _(eng):
                run("pool", eng)

            @block.sync
            def _(eng):
                run("sp", eng)


class Pool:
    def __init__(self, bufs):
        self.bufs = bufs
        self.i = 0

    def next(self):
        b = self.bufs[self.i % len(self.bufs)]
        self.i += 1
        return b


def build_program(debug=False):
    nc = bass.Bass("TRN2", target_bir_lowering=False)
    x_d = nc.dram_tensor("x", [S, D], F32, kind="ExternalInput").ap()
    v_d = nc.dram_tensor("vmat", [V_ROWS, 128], F32, kind="ExternalInput").ap()
    wada_d = nc.dram_tensor("wada", [12, 128, 4096], F32, kind="ExternalInput").ap()
    winc_d = nc.dram_tensor("winc", [4, 128, 2048], F32, kind="ExternalInput").ap()
    wins_d = nc.dram_tensor("wins", [4, 128, 3072], F32, kind="ExternalInput").ap()
    wout_d = nc.dram_tensor("wout", [2, 128, 4096], F32, kind="ExternalInput").ap()
    wgu_d = nc.dram_tensor("wgu", [11, 128, 4096], F32, kind="ExternalInput").ap()
    wdn_d = nc.dram_tensor("wdn", [8, 128, 2816], F32, kind="ExternalInput").ap()
    out_d = nc.dram_tensor("out", [S, D], F32, kind="ExternalOutput").ap()
    dbg_d = {}

    P = Prog(nc)
    es = ExitStack()
    with es:
        def sb(name, shape, dt):
            return es.enter_context(nc.sbuf_tensor(name, shape, dt))

        def ps(name):
            return es.enter_context(nc.psum_tensor(name, [128, 512], F32))

        def sem(name):
            return es.enter_context(nc.semaphore(name))

        esems = {e: sem("s_" + e) for e in ("pe", "act", "dve", "pool")}

        ident_f = sb("ident_f", [128, 128], F32)
        ident_b = sb("ident_b", [128, 128], BF16)
        ones_ln = sb("ones_ln", [128, 128], BF16)
        ones_d = sb("ones_d", [128, 128], BF16)
        blk = sb("blk", [128, 128], BF16)
        eps_t = sb("eps_t", [128, 1], F32)
        one_t = sb("one_t", [128, 1], F32)
        v_sb = sb("v_sb", [128, 2, 128], F32)
        vt = sb("vt", [128, V_ROWS], F32)
        nvt = sb("nvt", [128, 8], F32)
        ctmp = sb("ctmp", [128, 8], F32)
        modT = sb("modT", [128, 48], F32)
        gs = sb("gs", [128, 16], F32)
        gtg = sb("gtg", [128, 16], F32)
        cact_m = sb("cact_m", [128, 8, 128], BF16)
        onesb = sb("onesb", [128, 128], BF16)
        adscr = sb("adscr", [128, 128], F32)
        adsum = sb("adsum", [128, 4], F32)
        dg = sb("dg", [128, 2, 31, 128], BF16)
        dg3 = sb("dg3", [128, 4, 3, 128], BF16)
        ssq = sb("ssq", [128, NT], F32)
        rstd = sb("rstd", [128, NT], F32)

        r_const = P.reg("const")
        r_vsb = P.reg("v_sb")
        r_vt = P.reg("vt")
        r_cact = P.reg("cact")
        r_ctmp = P.reg("ctmp")
        r_mod = [P.reg("mod%d" % i) for i in range(6)]
        r_gs = [P.reg("gs0"), P.reg("gs1")]
        r_gtg = [P.reg("gtg0"), P.reg("gtg1")]
        r_adscr = P.reg("adscr")
        r_dg = [P.reg("dg0"), P.reg("dg1")]
        r_dg3 = P.reg("dg3")
        r_ssq = P.regs_n("ssq", NS)
        r_rstd = P.regs_n("rstd", NS)

        xb = sb("xb", [128, NT, D], F32)
        assert NS == 2
        hy = sb("hy", [128, 16, TB], BF16)
        hT = hy[:, 0:8, :]
        yT = hy[:, 8:16, :]
        hyf = hy[:].rearrange("p a t -> p (a t)").bitcast(F32).rearrange("p (s m t) -> p s m t", s=2, m=8)
        fT_F = lambda m, s: hyf[:, s, m, :]
        R = sb("R", [128, 11 * TB], F32)
        actT = R[:].bitcast(BF16).rearrange("p (j t) -> p j t", j=NJ)
        u1 = R[:, 0:4 * TB].rearrange("p (j t) -> p j t", j=4)
        o1 = 4 * TB
        cvb = R[:, o1:o1 + 4 + 2 * TB].bitcast(BF16).rearrange("p (j t) -> p j t", j=4)
        o2 = o1 + 4 + 2 * TB
        u0b = R[:, o2:o2 + 60 + 2 * TB].bitcast(BF16).rearrange("p (j t) -> p j t", j=4)
        o3 = o2 + 60 + 2 * TB
        xn_views = [R[:, o3 + 512 * i:o3 + 512 * (i + 1)].bitcast(BF16) for i in range(4)]
        o4 = o3 + 2048
        junk = R[:, o4:o4 + 512].bitcast(BF16)
        assert o4 + 512 <= 11 * TB
        fT_Cv = R[:, 0:8 * TB].rearrange("p (s m t) -> p s m t", s=2, m=8)
        fT_C = lambda m, s: fT_Cv[:, s, m, :]
        cvh_s = sb("cvh_s", [128, 4, 2], BF16)
        u0h_s = sb("u0h_s", [128, 4, 30], BF16)
        slots = [sb("slot%d" % i, [128, SLOT_F], BF16) for i in range(RING)]

        r_xb = P.regs_n("xb", NT)
        r_hT = [[P.reg("hT%d_%d" % (k, s)) for s in range(NS)] for k in range(8)]
        r_yT = [[P.reg("yT%d_%d" % (k, s)) for s in range(NS)] for k in range(8)]
        r_u1 = [[P.reg("u1%d_%d" % (j, s)) for s in range(NS)] for j in range(4)]
        r_cvb = [[P.reg("cvb%d_%d" % (j, s)) for s in range(NS)] for j in range(4)]
        r_cvh = P.regs_n("cvh", 4)
        r_cvhA = P.regs_n("cvhA", 4)
        r_u0hA = P.regs_n("u0hA", 4)
        r_u0b = [[P.reg("u0b%d_%d" % (j, s)) for s in range(NS)] for j in range(4)]
        r_u0h = P.regs_n("u0h", 4)
        r_act = [[P.reg("act%d_%d" % (j, s)) for s in range(NS)] for j in range(NJ)]
        r_fTC = [[P.reg("fTC%d_%d" % (m, s)) for s in range(NS)] for m in range(8)]
        r_fTF = [[P.reg("fTF%d_%d" % (m, s)) for s in range(NS)] for m in range(8)]
        r_slot = P.regs_n("slot", RING)

        def mkpool(name, n, shape, dt):
            return Pool([(sb("%s%d" % (name, i), shape, dt), P.reg("%s%d" % (name, i))) for i in range(n)])

        tf = mkpool("tf", 10, [128, 512], F32)
        tff = Pool(tf.bufs[2:10])
        tgb = mkpool("tgb", 2, [128, 512], F32)
        tb = mkpool("tb", 4, [128, 512], BF16)
        tb2 = mkpool("tb2", 4, [128, 512], BF16)
        txp = mkpool("tx", 1, [128, 512], F32)
        txn = Pool([(xn_views[i], P.reg("txn%d" % i)) for i in range(4)])
        r_junk = P.reg("junk")

        mmr = Pool([(ps("mm%d" % i), P.reg("mm%d" % i, True)) for i in range(4)])
        tpr = Pool([(ps("tp%d" % i), P.reg("tp%d" % i, True)) for i in range(2)])
        stb = [(ps("st%d" % i), P.reg("st%d" % i, True)) for i in range(2)]
        strr = Pool(stb)

        ds_x = [P.dsem(sem("dx%d" % i), "dx%d" % i) for i in range(NT)]
        ds_o = [P.dsem(sem("do%d" % i), "do%d" % i) for i in range(NT)]
        ds_slot = [P.dsem(sem("dw%d" % i), "dw%d" % i) for i in range(RING)]
        ds_v = P.dsem(sem("dv"), "dv")
        ds_dbg = P.dsem(sem("ddbg"), "ddbg")

        def flat(ll):
            return [r for l in ll for r in l]

        r_txn = [b[1] for b in txn.bufs]

        def vcol(row, n=1):
            return vt[:, row:row + n]

        def dump(name, ap, shape, dt, regs):
            if not debug:
                return
            t = nc.dram_tensor("dbg_" + name, list(shape), dt, kind="ExternalOutput").ap()
            dbg_d[name] = t
            P.op("sp", lambda e: e.dma_start(out=t, in_=ap), reads=regs, dsem=ds_dbg)

        sched = []
        for p in range(4):
            sched.append((wada_d[p], 4096))
        for T in range(NB):
            for j in range(4):
                sched.append((winc_d[j], 2048))
                if T == 0:
                    sched.append((wada_d[4 + j], 4096))
            for j in range(4):
                sched.append((wins_d[j], 3072))
                if T == 0:
                    sched.append((wada_d[8 + j], 4096))
            for h in range(2):
                sched.append((wout_d[h], 4096))
            for g in range(11):
                sched.append((wgu_d[g], 4096))
            for rpt in range(2 if T == NB - 1 else 1):
                for m in range(8):
                    sched.append((wdn_d[m], 2816))
        state = {"issued": 0, "next": 0, "released": 0, "hold": False}

        def prefetch():
            while state["issued"] < len(sched) and state["issued"] - RING < state["released"]:
                i = state["issued"]
                k = i % RING
                src, F = sched[i]
                P.op("pool", lambda e, k=k, src=src, F=F: e.dma_start(out=slots[k][:, 0:F], in_=src),
                     writes=[r_slot[k]], dsem=ds_slot[k], name="wload%d" % i)
                state["issued"] += 1

        def take(hold=False):
            i = state["next"]
            state["next"] += 1
            if hold:
                state["hold"] = True
            elif not state["hold"]:
                state["released"] = i
            prefetch()
            assert state["issued"] > i, (state, i)
            k = i % RING
            return slots[k], r_slot[k]

        def end_hold():
            state["hold"] = False
            state["released"] = state["next"]
            prefetch()

        P.op("sp", lambda e: e.dma_start(out=v_sb[:], in_=v_d.rearrange("(t r) p -> r t p", t=2)),
             writes=[r_vsb], dsem=ds_v)
        prefetch()

        r_idf = P.reg("ident_f")
        r_blk = P.reg("blk")
        P.op("pool", lambda e: e.memset(ident_f[:], 0.0), writes=[r_idf])
        P.op("pool", lambda e: e.affine_select(out=ident_f[:], in_=ident_f[:], compare_op=ALU.not_equal, fill=1.0,
                                               base=0, pattern=[[-1, 128]], channel_multiplier=1),
             reads=[r_idf], writes=[r_idf])
        P.op("pool", lambda e: e.memset(ones_ln[:], 1.0 / 512.0), writes=[P.reg("x1")])
        P.op("pool", lambda e: e.memset(ones_d[:], 1.0 / 1024.0), writes=[P.reg("x2")])
        P.op("pool", lambda e: e.memset(onesb[:], 1.0), writes=[P.reg("x5")])
        P.op("pool", lambda e: e.memset(blk[:], 0.0), writes=[r_blk])
        P.op("pool", lambda e: e.memset(blk[0:64, 0:64], 1.0 / 64.0), writes=[r_blk])
        P.op("pool", lambda e: e.memset(blk[64:128, 64:128], 1.0 / 64.0), writes=[r_blk])
        P.op("pool", lambda e: e.memset(eps_t[:], EPS), writes=[P.reg("x3")])
        P.op("pool", lambda e: e.memset(one_t[:], 1.0), writes=[P.reg("x4")])
        P.op("pool", lambda e: e.memset(cvh_s[:], 0.0), writes=r_cvh)
        P.op("pool", lambda e: e.memset(u0h_s[:], 0.0), writes=r_u0h)
        P.op("pool", lambda e: e.memset(ctmp[:], 0.0), reads=[r_idf, r_blk], writes=[r_const, r_ctmp])
        P.op("dve", lambda e: e.tensor_copy(out=ident_b[:], in_=ident_f[:]), reads=[r_const], writes=[r_const])

        st_t, st_r = strr.next()
        for t in range(2):
            P.op("pe", lambda e, t=t: e.transpose(out=st_t[:, t * 128:(t + 1) * 128], in_=v_sb[:, t, :], identity=ident_f[:]),
                 reads=[r_vsb, r_const], writes=[st_r])
        P.op("dve", lambda e: e.tensor_copy(out=vt[:], in_=st_t[:, 0:V_ROWS]), reads=[st_r], writes=[r_vt])
        P.op("dve", lambda e: e.tensor_scalar(out=nvt[:], in0=vt[:, V_GLN:V_GLN + 8], scalar1=-1.0, scalar2=None, op0=ALU.mult),
             reads=[r_vt], writes=[r_vt])
        P.op("act", lambda e: e.activation(out=ctmp[:], in_=vt[:, V_C:V_C + 8], func=AF.Exp, scale=-1.0), reads=[r_vt], writes=[r_ctmp])
        P.op("act", lambda e: e.activation(out=ctmp[:], in_=ctmp[:], func=AF.Ln, bias=one_t[:], scale=1.0), reads=[r_ctmp, r_const], writes=[r_ctmp])
        P.op("act", lambda e: e.activation(out=ctmp[:], in_=ctmp[:], func=AF.Exp, scale=-1.0), reads=[r_ctmp], writes=[r_ctmp])
        P.op("dve", lambda e: e.tensor_tensor(out=ctmp[:], in0=ctmp[:], in1=vt[:, V_C:V_C + 8], op=ALU.mult),
             reads=[r_ctmp, r_vt], writes=[r_ctmp])
        for kc in range(8):
            P.op("dve", lambda e, kc=kc: e.tensor_scalar(out=cact_m[:, kc, :], in0=onesb[:], scalar1=ctmp[:, kc:kc + 1],
                                                        scalar2=None, op0=ALU.mult),
                 reads=[r_ctmp, r_const], writes=[r_cact])
        for k in range(31):
            P.op("dve", lambda e, k=k: e.tensor_scalar(
                out=dg[:, 0, k, :], in0=ident_b[:], scalar1=vcol(V_WCF + k * 4), scalar2=None, op0=ALU.mult),
                reads=[r_const, r_vt], writes=[r_dg[0]])
        for j in range(4):
            for k in range(3):
                P.op("dve", lambda e, j=j, k=k: e.tensor_scalar(
                    out=dg3[:, j, k, :], in0=ident_b[:], scalar1=vcol(V_WSH + k * 4 + j), scalar2=None, op0=ALU.mult),
                    reads=[r_const, r_vt], writes=[r_dg3])

        def ada_piece(p):
            slot, rsl = take()
            w = slot[:, 0:4096].rearrange("p (k c) -> p k c", k=8)
            mt, mr = mmr.next()
            for kc in range(8):
                P.op("pe", lambda e, kc=kc: e.matmul(mt[:], lhsT=cact_m[:, kc, :], rhs=w[:, kc, :],
                                                    start=(kc == 0), stop=(kc == 7)),
                     reads=[r_cact, rsl], writes=[mr])
            for q in range(4):
                P.op("dve", lambda e, q=q: e.scalar_tensor_tensor(
                    out=adscr[:], in0=mt[:, q * 128:(q + 1) * 128], scalar=1.0, in1=ident_f[:],
                    op0=ALU.mult, op1=ALU.mult, accum_out=adsum[:, q:q + 1]),
                    reads=[mr, r_const], writes=[r_adscr])
            sec = p // 2
            P.op("dve", lambda e: e.tensor_tensor(out=modT[:, 4 * p:4 * p + 4], in0=adsum[:, 0:4],
                                                 in1=vt[:, V_BADA + 4 * p:V_BADA + 4 * p + 4], op=ALU.add),
                 reads=[r_adscr, r_vt], writes=[r_mod[sec]])

        def ada_derive_gs(l):
            sc = 1 + 3 * l
            grow = V_GPRE1 if l == 0 else V_GPRE2
            P.op("dve", lambda e: e.scalar_tensor_tensor(out=gs[:, 8 * l:8 * l + 8], in0=modT[:, 8 * sc:8 * sc + 8], scalar=1.0,
                                                        in1=vt[:, grow:grow + 8], op0=ALU.add, op1=ALU.mult),
                 reads=[r_mod[sc], r_vt], writes=[r_gs[l]])

        def ada_derive_gtg(l):
            gt = 2 + 3 * l
            grow = V_GPOST1 if l == 0 else V_GPOST2
            P.op("dve", lambda e: e.tensor_tensor(out=gtg[:, 8 * l:8 * l + 8], in0=modT[:, 8 * gt:8 * gt + 8],
                                                 in1=vt[:, grow:grow + 8], op=ALU.mult),
                 reads=[r_mod[gt], r_vt], writes=[r_gtg[l]])

        def rsqrt_ops(src_ap, src_regs, dst_ap, dst_regs, scale=1.0):
            P.op("act", lambda e: e.activation(out=dst_ap, in_=src_ap, func=AF.Ln, bias=eps_t[:], scale=scale),
                 reads=list(src_regs) + [r_const], writes=list(dst_regs))
            P.op("act", lambda e: e.activation(out=dst_ap, in_=dst_ap, func=AF.Exp, scale=-0.5),
                 reads=list(dst_regs), writes=list(dst_regs))

        def prenorm_steps(T, l, s):
            sh = 0 if l == 0 else 3
            xns = []

            def st_stats():
                for ii in range(4):
                    i = s * 4 + ii
                    P.op("act", lambda e, i=i: e.activation(out=junk[:], in_=xb[:, i, :], func=AF.Square,
                                                           accum_out=ssq[:, i:i + 1]),
                         reads=[r_xb[i]], writes=[r_junk, r_ssq[s]])
                rsqrt_ops(ssq[:, 4 * s:4 * s + 4], [r_ssq[s]], rstd[:, 4 * s:4 * s + 4], [r_rstd[s]], scale=1.0 / D)

            def st_xn():
                for ii in range(4):
                    i = s * 4 + ii
                    xn, xnr = txn.next()
                    xns.append((xn, xnr))
                    P.op("dve", lambda e, i=i, xn=xn: e.tensor_scalar(out=xn[:], in0=xb[:, i, :], scalar1=rstd[:, i:i + 1],
                                                                   scalar2=None, op0=ALU.mult),
                         reads=[r_xb[i], r_rstd[s]], writes=[xnr])

            def st_half(half):
                tps = [tpr.next(), tpr.next()]
                tviews = [t[0].bitcast(BF16) for t in tps]
                for ii in range(4):
                    xn, xnr = xns[ii]
                    for kq in range(4):
                        kc = half * 4 + kq
                        tv = tviews[kq // 2]
                        c0 = (kq % 2) * 512 + ii * 128
                        P.op("pe", lambda e, kc=kc, tv=tv, c0=c0, xn=xn: e.transpose(
                            out=tv[:, c0:c0 + 128], in_=xn[:, kc * 128:(kc + 1) * 128], identity=ident_b[:]),
                            reads=[xnr, r_const], writes=[tps[kq // 2][1]])
                for kq in range(4):
                    kc = half * 4 + kq
                    tv = tviews[kq // 2]
                    c0 = (kq % 2) * 512
                    if kq // 2 == 0:
                        P.op("act", lambda e, kc=kc, tv=tv, c0=c0: e.activation(
                            out=hT[:, kc, s * 512:(s + 1) * 512], in_=tv[:, c0:c0 + 512], func=AF.Identity,
                            bias=modT[:, 8 * sh + kc:8 * sh + kc + 1], scale=gs[:, 8 * l + kc:8 * l + kc + 1]),
                            reads=[tps[kq // 2][1], r_mod[sh], r_gs[l]], writes=[r_hT[kc][s]])
                    else:
                        P.op("dve", lambda e, kc=kc, tv=tv, c0=c0: e.tensor_scalar(
                            out=hT[:, kc, s * 512:(s + 1) * 512], in0=tv[:, c0:c0 + 512],
                            scalar1=gs[:, 8 * l + kc:8 * l + kc + 1], scalar2=modT[:, 8 * sh + kc:8 * sh + kc + 1],
                            op0=ALU.mult, op1=ALU.add),
                            reads=[tps[kq // 2][1], r_mod[sh], r_gs[l]], writes=[r_hT[kc][s]])

            return [st_stats, st_xn, lambda: st_half(0), lambda: st_half(1)]

        def prenorm_part(T, l, s):
            for th in prenorm_steps(T, l, s):
                th()

        def _old_prenorm_part(T, l, s):
            sh = 0 if l == 0 else 3
            for ii in range(4):
                i = s * 4 + ii
                P.op("act", lambda e, i=i: e.activation(out=junk[:], in_=xb[:, i, :], func=AF.Square,
                                                       accum_out=ssq[:, i:i + 1]),
                     reads=[r_xb[i]], writes=[r_junk, r_ssq[s]])
            rsqrt_ops(ssq[:, 4 * s:4 * s + 4], [r_ssq[s]], rstd[:, 4 * s:4 * s + 4], [r_rstd[s]], scale=1.0 / D)
            xns = []
            for ii in range(4):
                i = s * 4 + ii
                xn, xnr = txn.next()
                xns.append((xn, xnr))
                P.op("dve", lambda e, i=i, xn=xn: e.tensor_scalar(out=xn[:], in0=xb[:, i, :], scalar1=rstd[:, i:i + 1],
                                                               scalar2=None, op0=ALU.mult),
                     reads=[r_xb[i], r_rstd[s]], writes=[xnr])
            for half in range(2):
                tps = [tpr.next(), tpr.next()]
                tviews = [t[0].bitcast(BF16) for t in tps]
                for ii in range(4):
                    xn, xnr = xns[ii]
                    for kq in range(4):
                        kc = half * 4 + kq
                        tv = tviews[kq // 2]
                        c0 = (kq % 2) * 512 + ii * 128
                        P.op("pe", lambda e, kc=kc, tv=tv, c0=c0, xn=xn: e.transpose(
                            out=tv[:, c0:c0 + 128], in_=xn[:, kc * 128:(kc + 1) * 128], identity=ident_b[:]),
                            reads=[xnr, r_const], writes=[tps[kq // 2][1]])
                for kq in range(4):
                    kc = half * 4 + kq
                    tv = tviews[kq // 2]
                    c0 = (kq % 2) * 512
                    if kq // 2 == 0:
                        P.op("act", lambda e, kc=kc, tv=tv, c0=c0, s=s: e.activation(
                            out=hT[:, kc, s * 512:(s + 1) * 512], in_=tv[:, c0:c0 + 512], func=AF.Identity,
                            bias=modT[:, 8 * sh + kc:8 * sh + kc + 1], scale=gs[:, 8 * l + kc:8 * l + kc + 1]),
                            reads=[tps[kq // 2][1], r_mod[sh], r_gs[l]], writes=[r_hT[kc][s]])
                    else:
                        P.op("dve", lambda e, kc=kc, tv=tv, c0=c0, s=s: e.tensor_scalar(
                            out=hT[:, kc, s * 512:(s + 1) * 512], in0=tv[:, c0:c0 + 512],
                            scalar1=gs[:, 8 * l + kc:8 * l + kc + 1], scalar2=modT[:, 8 * sh + kc:8 * sh + kc + 1],
                            op0=ALU.mult, op1=ALU.add),
                            reads=[tps[kq // 2][1], r_mod[sh], r_gs[l]], writes=[r_hT[kc][s]])

        def prenorm(T, l):
            for s in range(NS):
                prenorm_part(T, l, s)

        def mm_group(out_t, out_r, lhs_fn, rhs_fn, nk, reads):
            for k in range(nk):
                P.op("pe", lambda e, k=k: e.matmul(out_t[:], lhsT=lhs_fn(k), rhs=rhs_fn(k), start=(k == 0), stop=(k == nk - 1)),
                     reads=reads, writes=[out_r])

        def head_norm(yf, yfr, c, s, ring=None, fpool=None):
            sq, sqr = tb.next()
            P.op("act", lambda e: e.activation(out=sq[:], in_=yf[:], func=AF.Square), reads=[yfr], writes=[sqr])
            st_t, st_r = (ring or strr).next()
            P.op("pe", lambda e: e.matmul(st_t[:], lhsT=blk[:], rhs=sq[:], start=True, stop=True),
                 reads=[sqr, r_const], writes=[st_r])
            rr, rrr = (fpool or tf).next()
            rsqrt_ops(st_t[:], [st_r], rr[:], [rrr])
            P.op("dve", lambda e: e.scalar_tensor_tensor(out=yT[:, c, s * 512:(s + 1) * 512], in0=yf[:],
                                                        scalar=vcol(V_BETA + c), in1=rr[:], op0=ALU.mult, op1=ALU.mult),
                 reads=[yfr, rrr, r_vt], writes=[r_yT[c][s]])

        def mixer(T):
            for j in range(4):
                P.op("dve", lambda e, j=j: e.tensor_copy(out=cvb[:, j, 0:2], in_=cvh_s[:, j, :]), reads=[r_cvh[j]], writes=[r_cvhA[j]])
                P.op("dve", lambda e, j=j: e.tensor_copy(out=u0b[:, j, 0:30], in_=u0h_s[:, j, :]), reads=[r_u0h[j]], writes=[r_u0hA[j]])
            st8 = {}

            def proj(w, rsl, c, s):
                mt, mr = mmr.next()
                mm_group(mt, mr, lambda k: w[:, c, k, :], lambda k: hT[:, k, s * 512:(s + 1) * 512], 8,
                         [rsl] + [r_hT[k][s] for k in range(8)])
                return mt, mr

            def front_c(j, s):
                if s == 0:
                    slot, rsl = take()
                    st8["w"] = slot[:, 0:2048].rearrange("p (c k m) -> p c k m", c=2, k=8)
                    st8["rsl"] = rsl
                    db = (T * 4 + j) % 2
                    st8["db"] = db
                    for k in range(31):
                        if T == 0 and j == 0:
                            break
                        P.op("dve", lambda e, k=k: e.tensor_scalar(
                            out=dg[:, db, k, :], in0=ident_b[:], scalar1=vcol(V_WCF + k * 4 + j), scalar2=None, op0=ALU.mult),
                            reads=[r_const, r_vt], writes=[r_dg[db]])
                w, rsl = st8["w"], st8["rsl"]
                c0 = s * 512
                gt_, gr = proj(w, rsl, 0, s)
                at, ar = proj(w, rsl, 1, s)
                sg, sgr = tf.next()
                P.op("act", lambda e: e.activation(out=sg[:], in_=gt_[:], func=AF.Exp, scale=-1.0), reads=[gr], writes=[sgr])
                P.op("act", lambda e: e.activation(out=sg[:], in_=sg[:], func=AF.Ln, bias=one_t[:], scale=1.0), reads=[sgr, r_const], writes=[sgr])
                P.op("act", lambda e: e.activation(out=sg[:], in_=sg[:], func=AF.Exp, scale=-1.0), reads=[sgr], writes=[sgr])
                P.op("dve", lambda e: e.tensor_tensor(out=u0b[:, j, 30 + c0:30 + c0 + 512], in0=sg[:], in1=at[:], op=ALU.mult),
                     reads=[sgr, ar], writes=[r_u0b[j][s]])
                if T == 0 and s == NS - 1:
                    ada_piece(4 + j)
                return (st8["db"],)

            def back_c(j, s, db):
                c0 = s * 512
                prev_u0 = r_u0b[j][s - 1] if s > 0 else r_u0hA[j]
                ct2, cr2 = mmr.next()
                mm_group(ct2, cr2, lambda k: dg[:, db, k, :], lambda k: u0b[:, j, c0 + k:c0 + k + 512], 31,
                         [r_dg[db], r_u0b[j][s], prev_u0])
                P.op("act", lambda e: e.activation(out=u1[:, j, c0:c0 + 512], in_=ct2[:], func=AF.Identity,
                                                   bias=vcol(V_BCF + j), scale=1.0),
                     reads=[cr2, r_vt], writes=[r_u1[j][s]])
                if s == NS - 1 and T + 1 < NB:
                    P.op("dve", lambda e: e.tensor_copy(out=u0h_s[:, j, :], in_=u0b[:, j, TB:TB + 30]),
                         reads=[r_u0b[j][NS - 1]], writes=[r_u0h[j]])

            pend = None
            for j in range(4):
                for s in range(NS):
                    cur = (j, s) + front_c(j, s)
                    if pend is not None:
                        back_c(*pend)
                    pend = cur
            back_c(*pend)

            tfs = Pool(tf.bufs[8:10])
            def front_s(j, s):
                if s == 0:
                    slot, rsl = take()
                    st8["w"] = slot[:, 0:3072].rearrange("p (c k m) -> p c k m", c=3, k=8)
                    st8["rsl"] = rsl
                w, rsl = st8["w"], st8["rsl"]
                c0 = s * 512
                gct, gcr = proj(w, rsl, 0, s)
                vtt, vr = proj(w, rsl, 1, s)
                gcs, gcsr = tfs.next()
                P.op("act", lambda e: e.copy(out=gcs[:], in_=gct[:]), reads=[gcr], writes=[gcsr])
                P.op("dve", lambda e: e.tensor_tensor(out=cvb[:, j, 2 + c0:2 + c0 + 512], in0=gcs[:], in1=vtt[:], op=ALU.mult),
                     reads=[gcsr, vr], writes=[r_cvb[j][s]])
                gbt, gbr = proj(w, rsl, 2, s)
                gbs, gbsr = tgb.next()
                P.op("act", lambda e: e.copy(out=gbs[:], in_=gbt[:]), reads=[gbr], writes=[gbsr])
                if T == 0 and s == NS - 1:
                    ada_piece(8 + j)
                return (gbs, gbsr)

            ysb = [tf.bufs[8], tf.bufs[9]]
            sqb = [tb2.bufs[2], tb2.bufs[3]]

            def back_a(n, j, s, gbs, gbsr):
                c0 = s * 512
                prev_cv = r_cvb[j][s - 1] if s > 0 else r_cvhA[j]
                ct, cr = mmr.next()
                mm_group(ct, cr, lambda k: dg3[:, j, k, :], lambda k: cvb[:, j, c0 + k:c0 + k + 512], 3,
                         [r_dg3, r_cvb[j][s], prev_cv])
                ysf, ysr = ysb[n % 2]
                P.op("dve", lambda e: e.scalar_tensor_tensor(
                    out=ysf[:], in0=ct[:], scalar=vcol(V_BSH + j), in1=gbs[:], op0=ALU.add, op1=ALU.mult),
                    reads=[cr, gbsr, r_vt], writes=[ysr])
                sq, sqr = sqb[n % 2]
                P.op("act", lambda e: e.activation(out=sq[:], in_=ysf[:], func=AF.Square), reads=[ysr], writes=[sqr])
                if s == NS - 1 and T + 1 < NB:
                    P.op("dve", lambda e: e.tensor_copy(out=cvh_s[:, j, :], in_=cvb[:, j, TB:TB + 2]),
                         reads=[r_cvb[j][NS - 1]], writes=[r_cvh[j]])

            def back_b(n, j, s):
                ysf, ysr = ysb[n % 2]
                sq, sqr = sqb[n % 2]
                st_t, st_r = mmr.next()
                P.op("pe", lambda e: e.matmul(st_t[:], lhsT=blk[:], rhs=sq[:], start=True, stop=True),
                     reads=[sqr, r_const], writes=[st_r])
                rr, rrr = txp.next()
                rsqrt_ops(st_t[:], [st_r], rr[:], [rrr])
                P.op("dve", lambda e: e.scalar_tensor_tensor(out=yT[:, j, s * 512:(s + 1) * 512], in0=ysf[:],
                                                            scalar=vcol(V_BETA + j), in1=rr[:], op0=ALU.mult, op1=ALU.mult),
                     reads=[ysr, rrr, r_vt], writes=[r_yT[j][s]])

            short_steps = []
            sst = {"pa": None, "pb": None, "n": 0}

            def sstep_parts(j, s):
                box = {}
                c0 = s * 512

                def half(c, lo, hi, post=None):
                    def th():
                        if c == 0 and lo == 0 and s == 0:
                            slot, rsl = take()
                            st8["w"] = slot[:, 0:3072].rearrange("p (c k m) -> p c k m", c=3, k=8)
                            st8["rsl"] = rsl
                        w, rsl = st8["w"], st8["rsl"]
                        if lo == 0:
                            box[c] = mmr.next()
                        mt, mr = box[c]
                        for k in range(lo, hi):
                            P.op("pe", lambda e, k=k: e.matmul(mt[:], lhsT=w[:, c, k, :], rhs=hT[:, k, s * 512:(s + 1) * 512],
                                                               start=(k == 0), stop=(k == 7)),
                                 reads=[rsl] + [r_hT[kk][s] for kk in range(8)], writes=[mr])
                        if post is not None:
                            post()
                    return th

                def post_v():
                    (gct, gcr), (vtt, vr) = box[0], box[1]
                    gcs, gcsr = txp.next()
                    P.op("act", lambda e: e.copy(out=gcs[:], in_=gct[:]), reads=[gcr], writes=[gcsr])
                    P.op("dve", lambda e: e.tensor_tensor(out=cvb[:, j, 2 + c0:2 + c0 + 512], in0=gcs[:], in1=vtt[:], op=ALU.mult),
                         reads=[gcsr, vr], writes=[r_cvb[j][s]])

                def post_gb():
                    gbt, gbr = box[2]
                    gbs, gbsr = tgb.next()
                    P.op("act", lambda e: e.copy(out=gbs[:], in_=gbt[:]), reads=[gbr], writes=[gbsr])
                    if T == 0 and s == NS - 1:
                        ada_piece(8 + j)
                    if sst["pa"] is not None:
                        back_a(*sst["pa"])
                    if sst["pb"] is not None:
                        back_b(*sst["pb"])
                    sst["pb"] = sst["pa"][0:3] if sst["pa"] is not None else None
                    sst["pa"] = (sst["n"], j, s, gbs, gbsr)
                    sst["n"] += 1

                return [half(0, 0, 4), half(0, 4, 8), half(1, 0, 4), half(1, 4, 8, post_v),
                        half(2, 0, 4), half(2, 4, 8, post_gb)]

            for j in range(4):
                for s in range(NS):
                    short_steps.extend(sstep_parts(j, s))
            def sflush():
                back_a(*sst["pa"])
                if sst["pb"] is not None:
                    back_b(*sst["pb"])
                back_b(*sst["pa"][0:3])
            short_steps.append(sflush)

            def ln_rows(s):
                c0 = s * 512
                if s == 0:
                    (mean_t, mean_r), (msq_t, msq_r) = stb[0], stb[1]
                else:
                    (mean_t, mean_r), (msq_t, msq_r) = tpr.next(), tpr.next()
                ubs = [tb.next() for _ in range(4)] if s == 0 else [tb2.bufs[j % 2] for j in range(4)]
                t1s = [(u1[:, j, c0:c0 + 512], r_u1[j][s]) for j in range(4)]
                ezs = [tf.bufs[4 * s + j] for j in range(4)]
                m2, m2r = ezs[0]
                rows = []

                def per_j(fn):
                    rows.append([(lambda j=j: fn(j)) for j in range(4)])

                def cast_p(j):
                    ub, ubr = ubs[j]
                    P.op("dve", lambda e: e.tensor_copy(out=ub[:], in_=u1[:, j, c0:c0 + 512]), reads=[r_u1[j][s]], writes=[ubr])

                def cast_c(j):
                    ub, ubr = ubs[j]
                    P.op("pe", lambda e: e.matmul(mean_t[:], lhsT=ones_ln[:], rhs=ub[:], start=(j == 0), stop=(j == 3)),
                         reads=[ubr, r_const], writes=[mean_r])

                def sq_p(j):
                    us, usr = ubs[j]
                    P.op("act", lambda e: e.activation(out=us[:], in_=u1[:, j, c0:c0 + 512], func=AF.Square), reads=[r_u1[j][s]], writes=[usr])

                def sq_c(j):
                    us, usr = ubs[j]
                    P.op("pe", lambda e: e.matmul(msq_t[:], lhsT=ones_ln[:], rhs=us[:], start=(j == 0), stop=(j == 3)),
                         reads=[usr, r_const], writes=[msq_r])

                rows.append([lambda: cast_p(0)] + [(lambda j=j: (cast_p(j), cast_c(j - 1))) for j in range(1, 4)])
                rows.append([lambda: (cast_c(3), sq_p(0))] + [(lambda j=j: (sq_p(j), sq_c(j - 1))) for j in range(1, 4)])

                def r_var():
                    P.op("act", lambda e: e.activation(out=m2[:], in_=mean_t[:], func=AF.Square), reads=[mean_r], writes=[m2r])
                    P.op("dve", lambda e: e.tensor_tensor(out=m2[:], in0=msq_t[:], in1=m2[:], op=ALU.subtract), reads=[msq_r, m2r], writes=[m2r])
                    rsqrt_ops(m2[:], [m2r], msq_t[:], [msq_r])
                rows.append([lambda: (sq_c(3), r_var())])

                def sub(j):
                    t1, t1r = t1s[j]
                    P.op("dve", lambda e: e.tensor_tensor(out=t1, in0=t1, in1=mean_t[:], op=ALU.subtract), reads=[t1r, mean_r], writes=[t1r])
                per_j(sub)

                def mul(j):
                    t1, t1r = t1s[j]
                    P.op("dve", lambda e: e.tensor_tensor(out=t1, in0=t1, in1=msq_t[:], op=ALU.mult), reads=[t1r, msq_r], writes=[t1r])
                per_j(mul)

                def e1(j):
                    t1, t1r = t1s[j]
                    ez, ezr = ezs[j]
                    P.op("act", lambda e: e.activation(out=ez[:], in_=t1, func=AF.Exp, bias=nvt[:, 4 + j:5 + j], scale=nvt[:, j:j + 1]),
                         reads=[t1r, r_vt], writes=[ezr])
                per_j(e1)

                def ln1(j):
                    t1, t1r = t1s[j]
                    ez, ezr = ezs[j]
                    P.op("act", lambda e: e.activation(out=ez[:], in_=ez[:], func=AF.Ln, bias=one_t[:], scale=1.0), reads=[ezr, r_const], writes=[ezr])
                    P.op("dve", lambda e: e.tensor_scalar(out=t1, in0=t1, scalar1=vcol(V_GLN + j), scalar2=vcol(V_BLN + j),
                                                          op0=ALU.mult, op1=ALU.add),
                         reads=[t1r, r_vt], writes=[t1r])
                per_j(ln1)

                def e2(j):
                    ez, ezr = ezs[j]
                    P.op("act", lambda e: e.activation(out=ez[:], in_=ez[:], func=AF.Exp, scale=-1.0), reads=[ezr], writes=[ezr])
                per_j(e2)

                def silu(j):
                    t1, t1r = t1s[j]
                    ez, ezr = ezs[j]
                    P.op("dve", lambda e: e.tensor_tensor(out=t1, in0=t1, in1=ez[:], op=ALU.mult), reads=[t1r, ezr], writes=[t1r])
                per_j(silu)

                def hsq_p(j):
                    t1, t1r = t1s[j]
                    sq_, sqr = ubs[j]
                    P.op("act", lambda e: e.activation(out=sq_[:], in_=t1, func=AF.Square), reads=[t1r], writes=[sqr])

                def hsq_c(j):
                    sq_, sqr = ubs[j]
                    P.op("pe", lambda e: e.matmul(mean_t[:], lhsT=blk[:], rhs=sq_[:], start=True, stop=True),
                         reads=[sqr, r_const], writes=[mean_r])
                    ez, ezr = ezs[j]
                    P.op("act", lambda e: e.activation(out=ez[:], in_=mean_t[:], func=AF.Ln, bias=eps_t[:], scale=1.0),
                         reads=[mean_r, r_const], writes=[ezr])

                def hexp(j):
                    ez, ezr = ezs[j]
                    P.op("act", lambda e: e.activation(out=ez[:], in_=ez[:], func=AF.Exp, scale=-0.5), reads=[ezr], writes=[ezr])

                rows.append([lambda: hsq_p(0)] + [(lambda j=j: (hsq_p(j), hsq_c(j - 1))) for j in range(1, 4)])
                rows.append([lambda: (hsq_c(3), hexp(0))] + [(lambda j=j: hexp(j)) for j in range(1, 4)])

                def out(j):
                    t1, t1r = t1s[j]
                    ez, ezr = ezs[j]
                    P.op("dve", lambda e: e.scalar_tensor_tensor(
                        out=yT[:, 4 + j, s * 512:(s + 1) * 512], in0=t1, scalar=vcol(V_BETA + 4 + j), in1=ez[:],
                        op0=ALU.mult, op1=ALU.mult),
                        reads=[t1r, ezr, r_vt], writes=[r_yT[4 + j][s]])
                per_j(out)
                return rows

            rows0 = ln_rows(0)
            rows1 = ln_rows(1)
            LAG = 0
            ln_list = []
            for k in range(len(rows0) + LAG):
                ra = rows0[k] if k < len(rows0) else []
                rb = rows1[k - LAG] if 0 <= k - LAG < len(rows1) else []
                for i in range(max(len(ra), len(rb))):
                    if i < len(ra):
                        ln_list.append(ra[i])
                    if i < len(rb):
                        ln_list.append(rb[i])
            interleave(ln_list, short_steps)

        def out_proj_parts(T, l, srcT, r_src, nk, fT, r_fT):
            ssq_b = [stb[s] for s in range(NS)]
            st = {"pend": None}

            def flush():
                if st["pend"] is not None:
                    st["pend"]()
                    st["pend"] = None

            def main(m, s, lhs, rsl):
                mt, mr = mmr.next()
                mm_group(mt, mr, lhs, lambda k: srcT[:, k, s * 512:(s + 1) * 512], nk,
                         [rsl] + [r_src[k][s] for k in range(nk)])
                P.op("act", lambda e: e.copy(out=fT(m, s), in_=mt[:]), reads=[mr], writes=[r_fT[m][s]])
                sq, sqr = tb.next()
                P.op("act", lambda e: e.activation(out=sq[:], in_=mt[:], func=AF.Square), reads=[mr], writes=[sqr])
                flush()
                st["pend"] = (lambda: P.op(
                    "pe", lambda e: e.matmul(ssq_b[s][0][:], lhsT=ones_d[:], rhs=sq[:], start=(m == 0), stop=(m == 7)),
                    reads=[sqr, r_const], writes=[ssq_b[s][1]]))

            def main_parts(m, s, lhs, rsl, parts=2):
                box = {}
                per = nk // parts
                ths = []
                for p in range(parts):
                    lo, hi = p * per, (nk if p == parts - 1 else (p + 1) * per)

                    def th(lo=lo, hi=hi, p=p):
                        if p == 0:
                            box["mt"], box["mr"] = mmr.next()
                        mt, mr = box["mt"], box["mr"]
                        for k in range(lo, hi):
                            P.op("pe", lambda e, k=k: e.matmul(mt[:], lhsT=lhs(k), rhs=srcT[:, k, s * 512:(s + 1) * 512],
                                                               start=(k == 0), stop=(k == nk - 1)),
                                 reads=[rsl] + [r_src[kk][s] for kk in range(nk)], writes=[mr])
                        if p == parts - 1:
                            P.op("act", lambda e: e.copy(out=fT(m, s), in_=mt[:]), reads=[mr], writes=[r_fT[m][s]])
                            sq, sqr = tb.next()
                            P.op("act", lambda e: e.activation(out=sq[:], in_=mt[:], func=AF.Square), reads=[mr], writes=[sqr])
                            flush()
                            st["pend"] = (lambda: P.op(
                                "pe", lambda e: e.matmul(ssq_b[s][0][:], lhsT=ones_d[:], rhs=sq[:], start=(m == 0), stop=(m == 7)),
                                reads=[sqr, r_const], writes=[ssq_b[s][1]]))
                    ths.append(th)
                return ths

            def tail(s):
                th = []
                th.append(lambda: rsqrt_ops(ssq_b[s][0][:], [ssq_b[s][1]], ssq_b[s][0][:], [ssq_b[s][1]]))
                tgp = [tf.bufs[0], tf.bufs[1]]
                tst = {"T": None, "A": None}

                def do_T(m):
                    tg, tgr = tgp[m % 2]
                    tp_t, tp_r = tpr.next()
                    for ii in range(4):
                        P.op("pe", lambda e, ii=ii: e.transpose(
                            out=tp_t[:, ii * 128:(ii + 1) * 128], in_=tg[:, ii * 128:(ii + 1) * 128], identity=ident_f[:]),
                            reads=[tgr, r_const], writes=[tp_r])
                    xv = xb[:, s * 4:s * 4 + 4, m * 128:(m + 1) * 128]
                    return (lambda: P.op(
                        "dve", lambda e: e.tensor_tensor(out=xv, in0=xv, in1=tp_t[:].rearrange("p (i c) -> p i c", i=4), op=ALU.add),
                        reads=[tp_r] + [r_xb[s * 4 + ii] for ii in range(4)], writes=[r_xb[s * 4 + ii] for ii in range(4)]))

                def advance():
                    newA = tst["T"]() if tst["T"] is not None else None
                    if tst["A"] is not None:
                        tst["A"]()
                    tst["A"] = newA
                    tst["T"] = None

                def step(m):
                    tg, tgr = tgp[m % 2]
                    P.op("dve", lambda e: e.scalar_tensor_tensor(
                        out=tg[:], in0=fT(m, s), scalar=gtg[:, 8 * l + m:8 * l + m + 1],
                        in1=ssq_b[s][0][:], op0=ALU.mult, op1=ALU.mult),
                        reads=[r_fT[m][s], r_gtg[l], ssq_b[s][1]], writes=[tgr])
                    advance()
                    tst["T"] = (lambda: do_T(m))

                for m in range(8):
                    th.append(lambda m=m: step(m))

                def fin():
                    advance()
                    advance()
                th.append(fin)
                return th

            return main, flush, tail, main_parts

        def ffn_group(w, rsl, g, jj, s):
            j = 2 * g + jj
            gt_, gr = mmr.next()
            mm_group(gt_, gr, lambda k: w[:, jj, 0, k, :], lambda k: hT[:, k, s * 512:(s + 1) * 512], 8,
                     [rsl] + [r_hT[k][s] for k in range(8)])
            ut, ur = mmr.next()
            mm_group(ut, ur, lambda k: w[:, jj, 1, k, :], lambda k: hT[:, k, s * 512:(s + 1) * 512], 8,
                     [rsl] + [r_hT[k][s] for k in range(8)])
            sl, slr = tff.next()
            P.op("act", lambda e: e.activation(out=sl[:], in_=gt_[:], func=AF.Silu), reads=[gr], writes=[slr])
            P.op("dve", lambda e: e.tensor_tensor(out=actT[:, j, s * 512:(s + 1) * 512], in0=sl[:], in1=ut[:], op=ALU.mult),
                 reads=[slr, ur], writes=[r_act[j][s]])

        def ffn_group_parts(w, rsl, g, jj, s):
            j = 2 * g + jj
            box = {}
            rd = [rsl] + [r_hT[k][s] for k in range(8)]

            def half(u, lo, hi, last=False):
                def th():
                    if lo == 0:
                        box[u] = mmr.next()
                    t_, r_ = box[u]
                    for k in range(lo, hi):
                        P.op("pe", lambda e, k=k: e.matmul(t_[:], lhsT=w[:, jj, u, k, :], rhs=hT[:, k, s * 512:(s + 1) * 512],
                                                           start=(k == 0), stop=(k == 7)),
                             reads=rd, writes=[r_])
                    if last:
                        (gt_, gr), (ut, ur) = box[0], box[1]
                        sl, slr = tff.next()
                        P.op("act", lambda e: e.activation(out=sl[:], in_=gt_[:], func=AF.Silu), reads=[gr], writes=[slr])
                        P.op("dve", lambda e: e.tensor_tensor(out=actT[:, j, s * 512:(s + 1) * 512], in0=sl[:], in1=ut[:], op=ALU.mult),
                             reads=[slr, ur], writes=[r_act[j][s]])
                return th
            return [half(0, 0, 4), half(0, 4, 8), half(1, 0, 4), half(1, 4, 8, last=True)]

        def interleave(a, b):
            na, nb = len(a), len(b)
            ia = ib = 0
            while ia < na or ib < nb:
                if ib >= nb or (ia < na and ia * nb <= ib * na):
                    a[ia]()
                    ia += 1
                else:
                    b[ib]()
                    ib += 1

        def load_x(T, tiles=None, after=()):
            for i in (range(NT) if tiles is None else tiles):
                r0 = (T * NT + i) * 128
                P.op("sp", lambda e, i=i, r0=r0: e.dma_start(out=xb[:, i, :], in_=x_d[r0:r0 + 128, :]),
                     reads=list(after), writes=[r_xb[i]], dsem=ds_x[i])

        def store_out(T):
            for i in range(NT):
                r0 = (T * NT + i) * 128
                P.op("sp", lambda e, i=i, r0=r0: e.dma_start(out=out_d[r0:r0 + 128, :], in_=xb[:, i, :]),
                     reads=[r_xb[i]], dsem=ds_o[i])

        try:
            load_x(0, tiles=range(0, 4))
            if STOP["at"] == "load":
                raise _StopBuild()
            for p in range(4):
                ada_piece(p)
            load_x(0, tiles=range(4, NT))
            ada_derive_gs(0)
            if STOP["at"] == "ada0":
                raise _StopBuild()
            prenorm(0, 0)
            for T in range(NB):
                if STOP["at"] == "prenorm0":
                    raise _StopBuild()
                if T == 0:
                    dump("h1T", hT[:], [128, 8, TB], BF16, [r for k in r_hT for r in k])
                mixer(T)
                if STOP["at"] == "mixer":
                    raise _StopBuild()
                if T == 0:
                    dump("yT", yT[:], [128, 8, TB], BF16, [r for k in r_yT for r in k])
                    ada_derive_gtg(0)
                    ada_derive_gs(1)
                    ada_derive_gtg(1)
                P.fence(flat(r_u1) + flat(r_cvb) + flat(r_u0b) + r_cvhA + r_u0hA, flat(r_fTC))
                mainC, flushC, tailC, mainC_parts = out_proj_parts(T, 0, yT, r_yT, 8, fT_C, r_fTC)
                pcs = [take(hold=True), take(hold=True)]

                def lhsC(m, pcs=pcs):
                    slot, rsl = pcs[m // 4]
                    w = slot[:, 0:4096].rearrange("p (m k c) -> p m k c", m=4, k=8)
                    return (lambda k: w[:, m % 4, k, :]), rsl

                for m in range(8):
                    mainC(m, 0, *lhsC(m))
                flushC()
                A = tailC(0) + prenorm_steps(T, 1, 0)
                B = [t for m in range(8) for t in mainC_parts(m, 1, *lhsC(m))] + [flushC]
                interleave(A, B)
                end_hold()
                NE = 3
                P.fence([r_fTC[m][0] for m in range(8)], [r_act[j][ss] for j in range(8) for ss in range(NS)])
                fp = [take(hold=True) for _ in range(NE)]
                fw = [(sl[:, 0:4096].rearrange("p (j u k m) -> p j u k m", j=2, u=2, k=8), rs) for sl, rs in fp]
                A2 = tailC(1) + prenorm_steps(T, 1, 1)
                B2 = [t for g in range(NE) for jj in range(2) for t in ffn_group_parts(fw[g][0], fw[g][1], g, jj, 0)]
                interleave(A2, B2)
                if T == 0:
                    dump("x1", xb[:], [128, NT, D], F32, r_xb)
                P.fence(flat(r_fTC) + r_txn + [r_junk], [r_act[j][ss] for j in range(8, NJ) for ss in range(NS)])
                for g in range(NE):
                    for jj in range(2):
                        ffn_group(fw[g][0], fw[g][1], g, jj, 1)
                end_hold()
                for g in range(NE, 11):
                    slot, rsl = take()
                    w = slot[:, 0:4096].rearrange("p (j u k m) -> p j u k m", j=2, u=2, k=8)
                    for jj in range(2):
                        for ss in range(NS):
                            ffn_group(w, rsl, g, jj, ss)
                if T == 0:
                    dump("actT", actT[:], [128, NJ, TB], BF16, [r for k in r_act for r in k])
                P.fence(flat(r_hT) + flat(r_yT), flat(r_fTF))

                def after_f(s, T=T):
                    for ii in range(4):
                        i = 4 * s + ii
                        r0 = (T * NT + i) * 128
                        P.op("sp", lambda e, i=i, r0=r0: e.dma_start(out=out_d[r0:r0 + 128, :], in_=xb[:, i, :]),
                             reads=[r_xb[i]], dsem=ds_o[i])
                    if T + 1 < NB:
                        for ii in range(4):
                            i = 4 * s + ii
                            r0 = ((T + 1) * NT + i) * 128
                            P.op("sp", lambda e, i=i, r0=r0: e.dma_start(out=xb[:, i, :], in_=x_d[r0:r0 + 128, :]),
                                 writes=[r_xb[i]], dsem=ds_x[i])

                mainF, flushF, tailF, mainF_parts = out_proj_parts(T, 1, actT, r_act, NJ, fT_F, r_fTF)
                if T + 1 < NB:
                    for m in range(8):
                        slot, rsl = take()
                        w = slot[:, 0:2816].rearrange("p (j c) -> p j c", j=NJ)
                        for ss in range(NS):
                            mainF(m, ss, (lambda k, w=w: w[:, k, :]), rsl)
                    flushF()
                    for ss in range(NS):
                        for th in tailF(ss):
                            th()
                        after_f(ss)
                else:
                    def mF(m, ss):
                        slot, rsl = take()
                        w = slot[:, 0:2816].rearrange("p (j c) -> p j c", j=NJ)
                        mainF(m, ss, (lambda k, w=w: w[:, k, :]), rsl)

                    for m in range(8):
                        mF(m, 0)
                    flushF()
                    def mF_parts(m, ss):
                        box = {}

                        def first():
                            slot, rsl = take()
                            box["w"] = slot[:, 0:2816].rearrange("p (j c) -> p j c", j=NJ)
                            box["ths"] = mainF_parts(m, ss, (lambda k: box["w"][:, k, :]), rsl, parts=4)
                            box["ths"][0]()
                        return [first] + [(lambda i=i: box["ths"][i]()) for i in range(1, 4)]

                    interleave(tailF(0) + [lambda: after_f(0)],
                               [t for m in range(8) for t in mF_parts(m, 1)] + [flushF])
                    for th in tailF(1):
                        th()
                    after_f(1)
                if T + 1 < NB:
                    P.fence(flat(r_act), r_txn + [r_junk] + flat(r_u1) + flat(r_cvb) + flat(r_u0b) + r_cvhA + r_u0hA)
                    P.fence(flat(r_fTF), flat(r_hT) + flat(r_yT))
                    for ss in range(NS):
                        prenorm_part(T + 1, 0, ss)

        except _StopBuild:
            store_out(0)
        if DEBUG.get("sbuf"):
            print("SBUF bytes remaining per partition:", nc.sbuf_bytes_remaining)
        P.emit(esems)
    return nc, dbg_d


def _layouts(inp):
    f = lambda a: np.ascontiguousarray(np.asarray(a, dtype=np.float32))
    w_ada = f(inp["w_ada"])[0]
    w_in = f(inp["w_in"])[0]
    w_out = f(inp["w_out"])[0]
    w_gu = f(inp["w_gate_up"])[0]
    w_dn = f(inp["w_down"])[0]
    wada = f(w_ada.reshape(8, 128, 12, 512).transpose(2, 1, 0, 3)).reshape(12, 128, 4096)
    t = w_in.reshape(8, 128, 5, 4, 128)[:, :, [1, 2, 0, 3, 4], :, :]
    tt = f(t.transpose(3, 1, 2, 0, 4))
    winc = f(tt[:, :, [4, 3]]).reshape(4, 128, 2048)
    wins = f(tt[:, :, 0:3]).reshape(4, 128, 3072)
    t = w_out.reshape(8, 128, 2, 4, 128)
    wout = f(t.transpose(2, 1, 3, 0, 4)).reshape(2, 128, 4096)
    t = w_gu.reshape(8, 128, 2, 11, 2, 128)
    wgu = f(t.transpose(3, 1, 4, 2, 0, 5)).reshape(11, 128, 4096)
    t = w_dn.reshape(NJ, 128, 8, 128)
    wdn = f(t.transpose(2, 1, 0, 3)).reshape(8, 128, 2816)
    return wada, winc, wins, wout, wgu, wdn


def _vmat(inp, b):
    rows = np.zeros((V_ROWS, 128), np.float32)

    def put(r0, a):
        a = np.asarray(a, dtype=np.float32).reshape(-1, 128)
        rows[r0:r0 + a.shape[0]] = a

    put(V_C, inp["c"][b])
    put(V_BADA, inp["b_ada"][0])
    put(V_GPRE1, inp["g_pre_mix"][0])
    put(V_GPOST1, inp["g_post_mix"][0])
    put(V_GPRE2, inp["g_pre_ffn"][0])
    put(V_GPOST2, inp["g_post_ffn"][0])
    put(V_BETA, inp["beta_mix"][0])
    put(V_WSH, inp["w_short"][0])
    put(V_BSH, inp["b_short"][0])
    put(V_WCF, inp["w_cfm_dw"][0])
    put(V_BCF, inp["b_cfm_dw"][0])
    put(V_GLN, inp["g_cfm_ln"][0])
    put(V_BLN, inp["b_cfm_ln"][0])
    return rows


_CACHE = {}


def kernel(**inputs):
    inp = {k: np.asarray(v) for k, v in inputs.items()}
    debug = bool(DEBUG.get("on"))
    key = ("nc", debug)
    if key not in _CACHE:
        _CACHE[key] = build_program(debug=debug)
    nc, dbg_d = _CACHE[key]
    wada, winc, wins, wout, wgu, wdn = _layouts(inp)
    x = np.asarray(inp["x"], dtype=np.float32)
    in_maps = []
    for b in range(NCORE):
        in_maps.append({
            "x": np.ascontiguousarray(x[b]),
            "vmat": _vmat(inp, b),
            "wada": wada, "winc": winc, "wins": wins, "wout": wout, "wgu": wgu, "wdn": wdn,
        })
    res = run_bass_kernel_spmd(nc, in_maps, core_ids=list(range(NCORE)))
    if debug:
        DEBUG["results"] = res.results
    out = np.stack([np.asarray(res.results[b]["out"], dtype=np.float32) for b in range(NCORE)], axis=0)
    return out
```
